# Optimizing a Trainium2 kernel written in Bass

```python
import math
import jax, jax.numpy as jnp
from jax import lax
import numpy as np

D_MODEL = 1024
BATCH = 4
SEQ = 8192
DEPTH = 2

N_MIXERS = 2
ATTN_HEADS = 8
ATTN_HEAD_DIM = D_MODEL // ATTN_HEADS
MOBA_BLOCK = 256
MOBA_TOPK = 3
Q_CHUNK = 128
REL_BUCKETS = 32
REL_MAX_DIST = 1024
GDN_HEADS = 8
GDN_HEAD_DIM = D_MODEL // GDN_HEADS
GDN_CONV = 4
GDN_CHUNK = 64
D_FF = 2816
FFN_CONV = 3
NORM_EPS = 1e-6
L2_EPS = 1e-6

kernel_name = "moba_gdn_convffn_hybrid"


def rmsnorm(x, g):
    xf = x.astype(jnp.float32)
    y = xf * lax.rsqrt(jnp.mean(xf * xf, axis=-1, keepdims=True) + NORM_EPS)
    return (y * g.astype(jnp.float32)).astype(x.dtype)


def l2norm(x):
    return x * lax.rsqrt(jnp.sum(x * x, axis=-1, keepdims=True) + L2_EPS)


def causal_dwconv(x, w):
    K, C = w.shape
    return lax.conv_general_dilated(
        x, w[:, None, :].astype(x.dtype), window_strides=(1,), padding=[(K - 1, 0)],
        dimension_numbers=("NWC", "WIO", "NWC"), feature_group_count=C)


def rel_bucket(dist):
    max_exact = REL_BUCKETS // 2
    d = jnp.maximum(dist, 0)
    large = max_exact + (jnp.log(jnp.maximum(d, 1).astype(jnp.float32) / max_exact)
                         / math.log(REL_MAX_DIST / max_exact) * (REL_BUCKETS - max_exact)).astype(jnp.int32)
    large = jnp.minimum(large, REL_BUCKETS - 1)
    return jnp.where(d < max_exact, d, large)


def moba_attention(x, w_qkv, w_o, rel_bias):
    B, S, _ = x.shape
    H, Dh, L = ATTN_HEADS, ATTN_HEAD_DIM, MOBA_BLOCK
    nb = -(-S // L)
    k_top = min(MOBA_TOPK, nb)
    n_chunks = S // Q_CHUNK
    C = Q_CHUNK
    scale = Dh ** -0.5
    f32 = jnp.float32
    q, k, v = jnp.split(x @ w_qkv, 3, axis=-1)
    heads = lambda t: t.reshape(B, S, H, Dh).transpose(0, 2, 1, 3)
    q, k, v = heads(q), heads(k), heads(v)
    pad = ((0, 0), (0, 0), (0, nb * L - S), (0, 0))
    kb = jnp.pad(k, pad).reshape(B, H, nb, L, Dh)
    vb = jnp.pad(v, pad).reshape(B, H, nb, L, Dh)
    k_mean = jnp.mean(kb.astype(f32), axis=3)
    head_idx = jnp.arange(H)[:, None, None, None]

    def one_chunk(bc):
        b, c = bc
        t0 = c * C
        j = t0 // L
        q_c = lax.dynamic_slice_in_dim(q[b], t0, C, axis=1)
        kb_b, vb_b = kb[b], vb[b]
        pos_q = t0 + jnp.arange(C)
        gate = jnp.einsum("hqd,hnd->hqn", q_c.astype(f32), k_mean[b])
        gate = jnp.where(jnp.arange(nb) < j, gate, -jnp.inf)
        _, sel = lax.top_k(gate, k_top)
        valid = jnp.arange(k_top) < j
        k_sel = jax.vmap(lambda blk, s: blk[s])(kb_b, sel)
        v_sel = jax.vmap(lambda blk, s: blk[s])(vb_b, sel)
        s_sel = jnp.einsum("hqd,hqnld->hqnl", q_c, k_sel, preferred_element_type=f32) * scale
        pos_sel = sel[..., None] * L + jnp.arange(L)
        b_sel = rel_bias[head_idx, rel_bucket(pos_q[None, :, None, None] - pos_sel)].astype(f32)
        s_sel = jnp.where(valid[None, None, :, None], s_sel + b_sel, -jnp.inf)
        k_own = lax.dynamic_index_in_dim(kb_b, j, axis=1, keepdims=False)
        v_own = lax.dynamic_index_in_dim(vb_b, j, axis=1, keepdims=False)
        dist_own = pos_q[:, None] - (j * L + jnp.arange(L))[None, :]
        s_own = (jnp.einsum("hqd,hld->hql", q_c, k_own, preferred_element_type=f32) * scale
                 + rel_bias[:, rel_bucket(dist_own)].astype(f32))
        s_own = jnp.where(dist_own[None] >= 0, s_own, -jnp.inf)
        p = jax.nn.softmax(jnp.concatenate([s_own, s_sel.reshape(H, C, k_top * L)], axis=-1), axis=-1)
        p_own = p[..., :L].astype(v.dtype)
        p_sel = p[..., L:].reshape(H, C, k_top, L).astype(v.dtype)
        o = (jnp.einsum("hql,hld->hqd", p_own, v_own, preferred_element_type=f32)
             + jnp.einsum("hqnl,hqnld->hqd", p_sel, v_sel, preferred_element_type=f32))
        return o.transpose(1, 0, 2).reshape(C, H * Dh).astype(x.dtype)

    b_idx = jnp.repeat(jnp.arange(B), n_chunks)
    c_idx = jnp.tile(jnp.arange(n_chunks), B)
    o = lax.map(one_chunk, (b_idx, c_idx)).reshape(B, S, H * Dh)
    return o @ w_o


def chunk_gated_delta_rule(q, k, v, g, beta):
    B, S, H, Dk = q.shape
    Dv = v.shape[-1]
    C = GDN_CHUNK
    N = S // C
    chunks = lambda t: t.reshape(B, N, C, H, t.shape[-1]).transpose(0, 3, 1, 2, 4)
    q, k, v = chunks(q), chunks(k), chunks(v)
    g = g.reshape(B, N, C, H).transpose(0, 3, 1, 2)
    beta = beta.reshape(B, N, C, H).transpose(0, 3, 1, 2)
    G = jnp.cumsum(g, axis=-1)
    idx = jnp.arange(C)
    causal = idx[:, None] >= idx[None, :]
    strict = idx[:, None] > idx[None, :]
    decay = jnp.exp(jnp.where(causal, G[..., :, None] - G[..., None, :], -jnp.inf))
    kk = jnp.einsum("bhncd,bhnsd->bhncs", k, k)
    low = jnp.where(strict, beta[..., None] * kk * decay, 0.0)
    eye = jnp.eye(C, dtype=q.dtype)
    rhs = jnp.concatenate([v * beta[..., None], k * (beta * jnp.exp(G))[..., None]], axis=-1)
    sol = lax.linalg.triangular_solve(eye + low, rhs, left_side=True, lower=True, unit_diagonal=True)
    u, w = sol[..., :Dv], sol[..., Dv:]
    attn = jnp.where(causal, jnp.einsum("bhncd,bhnsd->bhncs", q, k) * decay, 0.0)
    q_dec = q * jnp.exp(G)[..., None]
    k_dec = k * jnp.exp(G[..., -1:] - G)[..., None]
    g_last = jnp.exp(G[..., -1])

    def step(state, inp):
        u_i, w_i, qd_i, kd_i, a_i, gl_i = inp
        v_new = u_i - jnp.einsum("bhcd,bhde->bhce", w_i, state)
        out = jnp.einsum("bhcd,bhde->bhce", qd_i, state) + jnp.einsum("bhcs,bhse->bhce", a_i, v_new)
        state = state * gl_i[..., None, None] + jnp.einsum("bhcd,bhce->bhde", kd_i, v_new)
        return state, out

    xs = tuple(jnp.moveaxis(t, 2, 0) for t in (u, w, q_dec, k_dec, attn, g_last))
    state0 = jnp.zeros((B, H, Dk, Dv), jnp.float32)
    _, out = lax.scan(step, state0, xs)
    return out.transpose(1, 0, 3, 2, 4).reshape(B, S, H, Dv)


def gated_deltanet(x, w_in, conv_w, a_log, dt_bias, o_norm, w_o):
    B, S, _ = x.shape
    H, Dk = GDN_HEADS, GDN_HEAD_DIM
    Dv = Dk
    f32 = jnp.float32
    proj = x @ w_in
    n_qkv = 2 * H * Dk + H * Dv
    qkv = jax.nn.silu(causal_dwconv(proj[..., :n_qkv], conv_w)).astype(f32)
    z = proj[..., n_qkv:n_qkv + H * Dv].reshape(B, S, H, Dv).astype(f32)
    b_in = proj[..., n_qkv + H * Dv:n_qkv + H * Dv + H].astype(f32)
    a_in = proj[..., n_qkv + H * Dv + H:].astype(f32)
    q = qkv[..., :H * Dk].reshape(B, S, H, Dk)
    k = qkv[..., H * Dk:2 * H * Dk].reshape(B, S, H, Dk)
    v = qkv[..., 2 * H * Dk:].reshape(B, S, H, Dv)
    q = l2norm(q) * (Dk ** -0.5)
    k = l2norm(k)
    beta = jax.nn.sigmoid(b_in)
    g = -jnp.exp(a_log.astype(f32)) * jax.nn.softplus(a_in + dt_bias.astype(f32))
    o = chunk_gated_delta_rule(q, k, v, g, beta)
    o = rmsnorm(o, o_norm) * jax.nn.silu(z)
    return o.reshape(B, S, H * Dv).astype(x.dtype) @ w_o


def conv_glu_ffn(x, w_up, conv_w, conv_b, w_down):
    gate, val = jnp.split(x @ w_up, 2, axis=-1)
    gate = causal_dwconv(gate, conv_w) + conv_b
    return (jax.nn.silu(gate) * val) @ w_down


def setup_inputs(seed: int = 0) -> dict:
    key = jax.random.key(seed)
    ks = iter(jax.random.split(key, 32))
    f32 = jnp.float32
    D = D_MODEL
    n_attn = (DEPTH + N_MIXERS - 1) // N_MIXERS
    n_gdn = DEPTH // N_MIXERS
    gdn_in = 3 * GDN_HEADS * GDN_HEAD_DIM + GDN_HEADS * GDN_HEAD_DIM + 2 * GDN_HEADS

    def w(shape, fan_in):
        return jax.random.normal(next(ks), shape, f32) * fan_in ** -0.5

    def gain(shape):
        return 1.0 + 0.05 * jax.random.normal(next(ks), shape, f32)

    x = jax.random.normal(next(ks), (BATCH, SEQ, D), f32)
    rel_bias = 0.5 * jax.random.normal(next(ks), (ATTN_HEADS, REL_BUCKETS), f32)
    attn_norm = gain((n_attn, D))
    attn_w_qkv = w((n_attn, D, 3 * ATTN_HEADS * ATTN_HEAD_DIM), D)
    attn_w_o = w((n_attn, ATTN_HEADS * ATTN_HEAD_DIM, D), ATTN_HEADS * ATTN_HEAD_DIM)
    gdn_norm = gain((n_gdn, D))
    gdn_w_in = w((n_gdn, D, gdn_in), D)
    gdn_conv_w = w((n_gdn, GDN_CONV, 3 * GDN_HEADS * GDN_HEAD_DIM), GDN_CONV)
    gdn_a_log = jnp.log(jax.random.uniform(next(ks), (n_gdn, GDN_HEADS), f32, 1.0, 16.0))
    dt = jnp.exp(jax.random.uniform(next(ks), (n_gdn, GDN_HEADS), f32, math.log(1e-3), math.log(1e-1)))
    gdn_dt_bias = dt + jnp.log(-jnp.expm1(-dt))
    gdn_o_norm = gain((n_gdn, GDN_HEAD_DIM))
    gdn_w_o = w((n_gdn, GDN_HEADS * GDN_HEAD_DIM, D), GDN_HEADS * GDN_HEAD_DIM)
    ffn_norm = gain((DEPTH, D))
    ffn_w_up = w((DEPTH, D, 2 * D_FF), D)
    ffn_conv_w = w((DEPTH, FFN_CONV, D_FF), FFN_CONV)
    ffn_conv_b = 0.02 * jax.random.normal(next(ks), (DEPTH, D_FF), f32)
    ffn_w_down = w((DEPTH, D_FF, D), D_FF)
    final_norm = gain((D,))
    return {"x": x, "rel_bias": rel_bias,
            "attn_norm": attn_norm, "attn_w_qkv": attn_w_qkv, "attn_w_o": attn_w_o,
            "gdn_norm": gdn_norm, "gdn_w_in": gdn_w_in, "gdn_conv_w": gdn_conv_w,
            "gdn_a_log": gdn_a_log, "gdn_dt_bias": gdn_dt_bias, "gdn_o_norm": gdn_o_norm,
            "gdn_w_o": gdn_w_o,
            "ffn_norm": ffn_norm, "ffn_w_up": ffn_w_up, "ffn_conv_w": ffn_conv_w,
            "ffn_conv_b": ffn_conv_b, "ffn_w_down": ffn_w_down, "final_norm": final_norm}


def reference(x, rel_bias, attn_norm, attn_w_qkv, attn_w_o, gdn_norm, gdn_w_in, gdn_conv_w,
              gdn_a_log, gdn_dt_bias, gdn_o_norm, gdn_w_o, ffn_norm, ffn_w_up, ffn_conv_w,
              ffn_conv_b, ffn_w_down, final_norm):
    h = x
    for i in range(DEPTH):
        m = i // N_MIXERS
        if i % N_MIXERS == 0:
            h = h + moba_attention(rmsnorm(h, attn_norm[m]), attn_w_qkv[m], attn_w_o[m], rel_bias)
        else:
            h = h + gated_deltanet(rmsnorm(h, gdn_norm[m]), gdn_w_in[m], gdn_conv_w[m], gdn_a_log[m],
                                   gdn_dt_bias[m], gdn_o_norm[m], gdn_w_o[m])
        h = h + conv_glu_ffn(rmsnorm(h, ffn_norm[i]), ffn_w_up[i], ffn_conv_w[i], ffn_conv_b[i], ffn_w_down[i])
    return rmsnorm(h, final_norm)
```

```python
import contextlib
import numpy as np
import ml_dtypes
import concourse.bass as bass
import concourse.mybir as mybir
from concourse.bass_utils import run_bass_kernel_spmd

F32 = mybir.dt.float32
BF16 = mybir.dt.bfloat16
AF = mybir.ActivationFunctionType
ALU = mybir.AluOpType
AX = mybir.AxisListType

D = 1024
DFF = 2816
NFC = DFF // 128
SEQ = 8192
HALF = SEQ // 2
HALO = 128
EPS = 1e-6


class Sem:
    __slots__ = ("h", "total")

    def __init__(self, h):
        self.h = h
        self.total = 0


class Buf:
    __slots__ = ("name", "w", "r", "excl")

    def __init__(self, name="", excl=False):
        self.name = name
        self.w = None
        self.r = []
        self.excl = excl


class Eng:
    def __init__(self, name, e, sem):
        self.name = name
        self.e = e
        self.sem = sem
        self.seen = {}


class Ctx:
    def __init__(self, nc, es, n_dma_sems=12):
        self.nc = nc
        self.es = es
        self.eng = {}
        for name, e in (("pe", nc.tensor), ("act", nc.scalar), ("dve", nc.vector),
                        ("pool", nc.gpsimd), ("sp", nc.sync)):
            self.eng[name] = Eng(name, e, Sem(es.enter_context(nc.semaphore("s_" + name))))
        self.dsem = {}
        self.dpos = {}
        for q in ("sp", "pool", "act"):
            self.dsem[q] = [Sem(es.enter_context(nc.semaphore("d_%s%d" % (q, i)))) for i in range(n_dma_sems)]
            self.dpos[q] = 0
        self.same_engine_sync = True
        self.pfx = ""
        self.embed_waits = True
        self.max_embed = 1

    def _wait(self, E, toks, embed=False):
        need = {}
        for s, v in toks:
            if v > need.get(s, 0):
                need[s] = v
        todo = []
        for s, v in need.items():
            if E.seen.get(s, 0) >= v:
                continue
            if s is E.sem and (E.name == "pe" or not self.same_engine_sync):
                continue
            todo.append((s, v))
        last = []
        if embed and self.embed_waits:
            while todo and len(last) < self.max_embed:
                last.append(todo.pop())
        for s, v in todo:
            E.e.wait_ge(s.h, v)
            E.seen[s] = v
        for s, v in last:
            E.seen[s] = v
        return last

    @staticmethod
    def _deps(reads, writes):
        toks = []
        for b in reads:
            if b.w is not None:
                toks.append(b.w)
        for b in writes:
            if b.w is not None:
                toks.append(b.w)
            toks.extend(b.r)
        return toks

    @staticmethod
    def _mark(tok, reads, writes):
        for b in writes:
            b.w = tok
            b.r = []
        for b in reads:
            b.r.append(tok)

    def op(self, en, fn, reads=(), writes=(), inc=True):
        E = self.eng[en]
        if any(b.excl for b in reads):
            writes = list(writes) + [b for b in reads if b.excl]
            reads = [b for b in reads if not b.excl]
        last = self._wait(E, self._deps(reads, writes), embed=True)
        ins = fn(E.e)
        for s_, v_ in last or ():
            ins._wait_ge(s_.h, v_)
        tok = (E.sem, E.sem.total + 1)
        if inc:
            ins.then_inc(E.sem.h, 1)
            E.sem.total += 1
        self._mark(tok, reads, writes)
        return ins

    def dma(self, q, out, in_, reads=(), writes=(), **kw):
        E = self.eng[q]
        i = self.dpos[q]
        self.dpos[q] = (i + 1) % len(self.dsem[q])
        s = self.dsem[q][i]
        toks = self._deps(reads, writes)
        if s.total > 0:
            toks.append((s, s.total))
        self._wait(E, toks)
        E.e.dma_start(out=out, in_=in_, **kw).then_inc(s.h, 16)
        s.total += 16
        self._mark((s, s.total), reads, writes)

    def barrier(self, final=False):
        toks = []
        for E in self.eng.values():
            if E.sem.total:
                toks.append((E.sem, E.sem.total))
        for q in self.dsem:
            for s in self.dsem[q]:
                if s.total:
                    toks.append((s, s.total))
        names = ("sp",) if final else tuple(self.eng)
        for n in names:
            E = self.eng[n]
            self._wait(E, [t for t in toks if t[0] is not E.sem])

    def sb(self, name, shape, dt):
        return self.es.enter_context(self.nc.sbuf_tensor("sb_" + self.pfx + name, list(shape), dt))

    def ps(self, name, shape, dt):
        return self.es.enter_context(self.nc.psum_tensor("ps_" + self.pfx + name, list(shape), dt))


def bufs(prefix, n, excl=False):
    return [Buf("%s%d" % (prefix, i), excl) for i in range(n)]


def phase_post(cx, io, final, n_tok=HALF, T=256):
    nc = cx.nc
    hin_main, hin_halo, oT_full, hout = io["hin_main"], io["hin_halo"], io["oT_full"], io["hout"]
    halo_mask = io.get("halo_mask", False)
    half = io.get("half", HALF)
    w_o, w_up, w_down = io["w_o"], io["w_up"], io["w_down"]

    wo_sb = cx.sb("wo_sb", [128, 8, D], BF16)
    wup_sb = cx.sb("wup_sb", [128, 8, 2 * DFF], BF16)
    wdn_sb = cx.sb("wdn_sb", [128, NFC, D], BF16)
    gcol = cx.sb("gcol", [128, 8], F32)
    cw = cx.sb("cw", [128, NFC, 3], F32)
    cb = cx.sb("cb", [128, NFC], F32)
    ident = cx.sb("ident", [128, 128], BF16)
    m01 = cx.sb("m01", [128, 2], F32)
    B_small = Buf("small")
    cx.dma("sp", m01[:], io["m01"], writes=[B_small])
    cx.dma("sp", gcol[:], io["gcol"], writes=[B_small])
    cx.dma("sp", cw[:], io["cw"], writes=[B_small])
    cx.dma("sp", cb[:], io["cb"], writes=[B_small])
    cx.dma("sp", ident[:], io["ident"], writes=[B_small])
    if final:
        gfin = cx.sb("gfin", [128, D], F32)
        cx.dma("sp", gfin[:], io["gfin"].partition_broadcast(128), writes=[B_small])

    SW = 1408
    NST = 3
    st_es = contextlib.ExitStack()
    stage = [st_es.enter_context(nc.sbuf_tensor("sb_%sstage%d" % (cx.pfx, i), [128, SW], F32)) for i in range(NST)]
    B_stage = bufs("stage", NST)
    B_wo, B_wup, B_wdn = Buf("wo"), Buf("wup"), Buf("wdn")
    k = 0
    cast_eng = ("act", "dve")

    def load_cast(src_ap, dst_ap, width, scale_ap, Bw):
        nonlocal k
        st, Bs = stage[k % NST], B_stage[k % NST]
        en = cast_eng[k % len(cast_eng)]
        k += 1
        cx.dma("sp", st[:, :width], src_ap, writes=[Bs])
        rd = [Bs, B_small] if scale_ap is not None else [Bs]
        if scale_ap is None:
            if en == "act":
                cx.op(en, lambda e: e.copy(out=dst_ap, in_=st[:, :width]), reads=rd, writes=[])
            else:
                cx.op(en, lambda e: e.tensor_copy(out=dst_ap, in_=st[:, :width]), reads=rd, writes=[])
        else:
            if en == "act":
                cx.op(en, lambda e: e.activation(out=dst_ap, in_=st[:, :width], func=AF.Copy, scale=scale_ap),
                      reads=rd, writes=[])
            else:
                cx.op(en, lambda e: e.tensor_scalar(out=dst_ap, in0=st[:, :width], scalar1=scale_ap, scalar2=None,
                                                    op0=ALU.mult), reads=rd, writes=[])
        E = cx.eng[en]
        Bw.r.append((E.sem, E.sem.total))

    for c in range(8):
        load_cast(w_o[c * 128:(c + 1) * 128, :], wo_sb[:, c, :], D, None, B_wo)
    for c in range(8):
        for j in range(2 * DFF // SW):
            load_cast(w_up[c * 128:(c + 1) * 128, j * SW:(j + 1) * SW], wup_sb[:, c, j * SW:(j + 1) * SW], SW,
                      gcol[:, c:c + 1], B_wup)
    for c in range(NFC):
        load_cast(w_down[c * 128:(c + 1) * 128, :], wdn_sb[:, c, :], D, None, B_wdn)

    def weights_ready(en, Bw):
        cx._wait(cx.eng[en], Bw.r)

    for en in ("act", "dve", "sp"):
        cx._wait(cx.eng[en], B_wo.r + B_wup.r + B_wdn.r)
    st_es.close()

    S = T // 128
    oT_sb = [cx.sb("oT_sb%d" % i, [128, 8, T], BF16) for i in range(2)]
    B_oT = bufs("oT", 2)
    h_sb = [cx.sb("h_sb%d" % i, [128, S, D], F32) for i in range(2)]
    B_h = bufs("h", 2)
    oA = cx.sb("oA", [128, 8, T], BF16)
    B_oA = Buf("oA")
    xn_bf = cx.sb("xn_bf", [128, D], BF16)
    B_xn = Buf("xn")
    junk, B_junk = xn_bf, B_xn
    ss = cx.sb("ss", [128, 8], F32)
    rstd = cx.sb("rstd", [128, 8], F32)
    B_ss = Buf("ss")
    xnT = cx.sb("xnT", [128, 8, T], BF16)
    B_xnT = Buf("xnT")
    gbuf = [cx.sb("gbuf%d" % i, [128, T + 2], F32) for i in range(2)]
    B_g = bufs("g", 2)
    carry = cx.sb("carry", [128, NFC, 2], F32)
    B_carry = [Buf("carry%d" % f) for f in range(NFC)]
    t1 = [cx.sb("t1_%d" % i, [128, T], F32) for i in range(2)]
    B_t1 = bufs("t1", 2)
    t2 = [cx.sb("t2_%d" % i, [128, T], F32) for i in range(2)]
    B_t2 = bufs("t2", 2)
    prodT = cx.sb("prodT", [128, NFC, T], BF16)
    B_prod = Buf("prod")

    pg = [cx.ps("pg%d" % i, [128, 512], F32) for i in range(2)]
    B_pg = bufs("pg", 2, True)
    pv = [cx.ps("pv%d" % i, [128, 512], F32) for i in range(2)]
    B_pv = bufs("pv", 2, True)
    pp = [cx.ps("pp%d" % i, [128, 512], F32) for i in range(2)]
    B_pp = bufs("pp", 2, True)
    psT = cx.ps("psT", [128, 8, 128], BF16)
    B_psT = Buf("psT", True)

    cx.op("dve", lambda e: e.memset(carry[:], 0.0), writes=B_carry)

    tiles = [(0, HALO, True)] + [(HALO + i * T, T, False) for i in range(n_tok // T)]
    npp = 0
    nf = 0

    def load_tile(ti):
        t0, tt, halo = tiles[ti]
        o_t, Bo = oT_sb[ti % 2], B_oT[ti % 2]
        h_t, Bh = h_sb[ti % 2], B_h[ti % 2]
        if halo:
            cx.dma("sp", o_t[:, :, :tt], oT_full[:, half - HALO:half].rearrange("(c p) t -> p c t", p=128),
                   writes=[Bo])
            cx.dma("sp", h_t[:, 0, :], hin_halo, writes=[Bh])
        else:
            p0 = t0 - HALO
            cx.dma("sp", oA[:, :, :tt], oT_full[:, p0:p0 + tt].rearrange("(c p) t -> p c t", p=128), writes=[B_oA])
            cx.dma("sp", o_t[:, :, :tt], oT_full[:, half + p0:half + p0 + tt].rearrange("(c p) t -> p c t", p=128),
                   writes=[Bo])
            cx.dma("sp", h_t[:, :tt // 128, :], hin_main[p0:p0 + tt, :].rearrange("(s p) d -> p s d", p=128),
                   writes=[Bh])

    def blend_tile(ti):
        t0, tt, halo = tiles[ti]
        o_t, Bo = oT_sb[ti % 2], B_oT[ti % 2]
        h_t, Bh = h_sb[ti % 2], B_h[ti % 2]
        if halo:
            cx.op("act", lambda e: e.activation(out=o_t[:, :, :tt], in_=o_t[:, :, :tt], func=AF.Copy,
                                                scale=m01[:, 1:2]), reads=[B_small], writes=[Bo])
            if halo_mask:
                cx.op("dve", lambda e: e.tensor_scalar(out=h_t[:, 0, :], in0=h_t[:, 0, :], scalar1=m01[:, 1:2],
                                                       scalar2=None, op0=ALU.mult), reads=[B_small], writes=[Bh])
        else:
            cx.op("act", lambda e: e.activation(out=oA[:, :, :tt], in_=oA[:, :, :tt], func=AF.Copy,
                                                scale=m01[:, 0:1]), reads=[B_small], writes=[B_oA])
            cx.op("dve", lambda e: e.scalar_tensor_tensor(out=o_t[:, :, :tt], in0=o_t[:, :, :tt],
                                                          scalar=m01[:, 1:2], in1=oA[:, :, :tt],
                                                          op0=ALU.mult, op1=ALU.add),
                  reads=[B_oA, B_small], writes=[Bo])

    def pro1(ti):
        nonlocal npp
        t0, tt, halo = tiles[ti]
        o_t, Bo = oT_sb[ti % 2], B_oT[ti % 2]
        h_t, Bh = h_sb[ti % 2], B_h[ti % 2]
        ns = tt // 128
        for s in range(ns):
            for hf in range(2):
                P, Bp = pp[npp % 2], B_pp[npp % 2]
                npp += 1
                for c in range(8):
                    cx.op("pe", lambda e, c=c: e.matmul(P[:, :], o_t[:, c, s * 128:(s + 1) * 128],
                                                       wo_sb[:, c, hf * 512:(hf + 1) * 512],
                                                       start=(c == 0), stop=(c == 7)),
                          reads=[Bo], writes=[Bp], inc=(c == 7))
                cx.op("dve", lambda e: e.tensor_tensor(out=h_t[:, s, hf * 512:(hf + 1) * 512],
                                                      in0=h_t[:, s, hf * 512:(hf + 1) * 512], in1=P[:, :],
                                                      op=ALU.add), reads=[Bp], writes=[Bh])
        for s in range(ns):
            cx.op("act", lambda e: e.activation(out=junk[:], in_=h_t[:, s, :], func=AF.Square,
                                                accum_out=ss[:, s:s + 1]), reads=[Bh], writes=[B_junk, B_ss])
        cx.op("dve", lambda e: e.tensor_scalar(out=rstd[:, :ns], in0=ss[:, :ns], scalar1=1.0 / D, scalar2=EPS,
                                               op0=ALU.mult, op1=ALU.add), reads=[B_ss], writes=[B_ss])
        cx.op("act", lambda e: e.sqrt(out=rstd[:, :ns], in_=rstd[:, :ns]), reads=[B_ss], writes=[B_ss])
        cx.op("dve", lambda e: e.reciprocal(out=rstd[:, :ns], in_=rstd[:, :ns]), reads=[B_ss], writes=[B_ss])

    def pro2(ti):
        t0, tt, halo = tiles[ti]
        o_t, Bo = oT_sb[ti % 2], B_oT[ti % 2]
        h_t, Bh = h_sb[ti % 2], B_h[ti % 2]
        ns = tt // 128
        for s in range(ns):
            cx.op("act", lambda e: e.activation(out=xn_bf[:], in_=h_t[:, s, :], func=AF.Copy,
                                                scale=rstd[:, s:s + 1]), reads=[Bh, B_ss], writes=[B_xn])
            for c in range(8):
                cx.op("pe", lambda e, c=c: e.transpose(psT[:, c, :], xn_bf[:, c * 128:(c + 1) * 128], ident[:]),
                      reads=[B_xn, B_small], writes=[B_psT], inc=(c == 7))
            cx.op("act", lambda e: e.copy(out=xnT[:, :, s * 128:(s + 1) * 128], in_=psT[:, :, :]),
                  reads=[B_psT], writes=[B_xnT])

    def upproj(ti):
        nonlocal nf
        t0, tt, halo = tiles[ti]
        o_t, Bo = oT_sb[ti % 2], B_oT[ti % 2]
        h_t, Bh = h_sb[ti % 2], B_h[ti % 2]
        ns = tt // 128
        for f in range(NFC):
            G, Bg = pg[nf % 2], B_pg[nf % 2]
            V, Bv = pv[nf % 2], B_pv[nf % 2]
            gb, Bgb = gbuf[nf % 2], B_g[nf % 2]
            a1, Ba1 = t1[nf % 2], B_t1[nf % 2]
            a2, Ba2 = t2[nf % 2], B_t2[nf % 2]
            nf += 1
            for c in range(8):
                cx.op("pe", lambda e, c=c: e.matmul(G[:, :tt], wup_sb[:, c, f * 128:(f + 1) * 128], xnT[:, c, :tt],
                                                   start=(c == 0), stop=(c == 7)),
                      reads=[B_xnT], writes=[Bg], inc=(c == 7))
            if not halo:
                for c in range(8):
                    cx.op("pe", lambda e, c=c: e.matmul(V[:, :tt], wup_sb[:, c, DFF + f * 128:DFF + (f + 1) * 128],
                                                       xnT[:, c, :tt], start=(c == 0), stop=(c == 7)),
                          reads=[B_xnT], writes=[Bv], inc=(c == 7))
            cx.op("act", lambda e: e.copy(out=gb[:, 0:2], in_=carry[:, f, :]),
                  reads=[B_carry[f]], writes=[Bgb])
            cx.op("act", lambda e: e.copy(out=gb[:, 2:2 + tt], in_=G[:, :tt]), reads=[Bg], writes=[Bgb])
            cx.op("act", lambda e: e.copy(out=carry[:, f, :], in_=gb[:, tt:tt + 2]),
                  reads=[Bgb], writes=[B_carry[f]])
            if halo:
                continue
            cx.op("dve", lambda e: e.tensor_scalar(out=a1[:, :tt], in0=gb[:, 2:2 + tt], scalar1=cw[:, f, 2:3],
                                                   scalar2=cb[:, f:f + 1], op0=ALU.mult, op1=ALU.add),
                  reads=[Bgb, B_small], writes=[Ba1])
            cx.op("dve", lambda e: e.scalar_tensor_tensor(out=a1[:, :tt], in0=gb[:, 1:1 + tt], scalar=cw[:, f, 1:2],
                                                          in1=a1[:, :tt], op0=ALU.mult, op1=ALU.add),
                  reads=[Bgb, Ba1], writes=[Ba1])
            cx.op("dve", lambda e: e.scalar_tensor_tensor(out=a1[:, :tt], in0=gb[:, 0:tt], scalar=cw[:, f, 0:1],
                                                          in1=a1[:, :tt], op0=ALU.mult, op1=ALU.add),
                  reads=[Bgb, Ba1], writes=[Ba1])
            cx.op("act", lambda e: e.activation(out=a2[:, :tt], in_=a1[:, :tt], func=AF.Silu),
                  reads=[Ba1], writes=[Ba2])
            cx.op("dve", lambda e: e.tensor_tensor(out=prodT[:, f, :tt], in0=a2[:, :tt], in1=V[:, :tt], op=ALU.mult),
                  reads=[Ba2, Bv], writes=[B_prod])

    def downproj(ti):
        nonlocal npp
        t0, tt, halo = tiles[ti]
        o_t, Bo = oT_sb[ti % 2], B_oT[ti % 2]
        h_t, Bh = h_sb[ti % 2], B_h[ti % 2]
        ns = tt // 128
        for s in range(ns):
            for hf in range(2):
                P, Bp = pp[npp % 2], B_pp[npp % 2]
                npp += 1
                for f in range(NFC):
                    cx.op("pe", lambda e, f=f: e.matmul(P[:, :], prodT[:, f, s * 128:(s + 1) * 128],
                                                       wdn_sb[:, f, hf * 512:(hf + 1) * 512],
                                                       start=(f == 0), stop=(f == NFC - 1)),
                          reads=[B_prod], writes=[Bp], inc=(f == NFC - 1))
                cx.op("dve", lambda e: e.tensor_tensor(out=h_t[:, s, hf * 512:(hf + 1) * 512],
                                                      in0=h_t[:, s, hf * 512:(hf + 1) * 512], in1=P[:, :],
                                                      op=ALU.add), reads=[Bp], writes=[Bh])

    def output(ti):
        t0, tt, halo = tiles[ti]
        o_t, Bo = oT_sb[ti % 2], B_oT[ti % 2]
        h_t, Bh = h_sb[ti % 2], B_h[ti % 2]
        ns = tt // 128
        r0 = t0 - HALO
        if final:
            for s in range(ns):
                cx.op("act", lambda e: e.activation(out=junk[:], in_=h_t[:, s, :], func=AF.Square,
                                                    accum_out=ss[:, 4 + s:5 + s]),
                      reads=[Bh], writes=[B_junk, B_ss])
            cx.op("dve", lambda e: e.tensor_scalar(out=rstd[:, 4:4 + ns], in0=ss[:, 4:4 + ns], scalar1=1.0 / D,
                                                   scalar2=EPS, op0=ALU.mult, op1=ALU.add),
                  reads=[B_ss], writes=[B_ss])
            cx.op("act", lambda e: e.sqrt(out=rstd[:, 4:4 + ns], in_=rstd[:, 4:4 + ns]), reads=[B_ss], writes=[B_ss])
            cx.op("dve", lambda e: e.reciprocal(out=rstd[:, 4:4 + ns], in_=rstd[:, 4:4 + ns]),
                  reads=[B_ss], writes=[B_ss])
            for s in range(ns):
                cx.op("dve", lambda e: e.scalar_tensor_tensor(out=h_t[:, s, :], in0=h_t[:, s, :],
                                                              scalar=rstd[:, 4 + s:5 + s], in1=gfin[:],
                                                              op0=ALU.mult, op1=ALU.mult),
                      reads=[B_ss, B_small], writes=[Bh])
        cx.dma("sp", hout[r0:r0 + tt, :].rearrange("(s p) d -> p s d", p=128), h_t[:, :ns, :], reads=[Bh])

    load_tile(0)
    blend_tile(0)
    weights_ready("pe", B_wo)
    pro1(0)
    pro2(0)
    weights_ready("pe", B_wup)
    for ti in range(len(tiles)):
        halo = tiles[ti][2]
        more = ti + 1 < len(tiles)
        if more:
            load_tile(ti + 1)
        upproj(ti)
        if more:
            blend_tile(ti + 1)
            pro1(ti + 1)
        if not halo:
            if ti == 1:
                weights_ready("pe", B_wdn)
            downproj(ti)
            output(ti)
        if more:
            pro2(ti + 1)


def post_small_inputs(ffn_norm, conv_w, conv_b):
    return {
        "gcol": np.ascontiguousarray(ffn_norm.reshape(8, 128).T),
        "cw": np.ascontiguousarray(conv_w.reshape(3, NFC, 128).transpose(2, 1, 0)),
        "cb": np.ascontiguousarray(conv_b.reshape(NFC, 128).T),
        "ident": np.eye(128, dtype=ml_dtypes.bfloat16),
    }


NEG = -30000.0
HG = 4
TTW = 1920


def attn_scratch(nc, S_):
    def dt_(name, shape, dt):
        return nc.dram_tensor(name, list(shape), dt).ap()
    return {
        "QT_d": dt_("QT_d", [HG, 128, S_], BF16), "KT_d": dt_("KT_d", [HG, 128, S_], BF16),
        "V_d": dt_("V_d", [HG, 128, S_ // 128, 128], BF16), "MT_d": dt_("MT_d", [HG, 32, S_], BF16),
    }


def phase_attn_a(cx, io, S_=SEQ):
    nc = cx.nc
    x = io["x"]
    QT_d, KT_d, V_d, MT_d = io["QT_d"], io["KT_d"], io["V_d"], io["MT_d"]
    T = 512
    ntile = S_ // T

    w_sb = cx.sb("wqkv_sb", [128, 8, 3 * 512], BF16)
    gcol = cx.sb("agcol", [128, 8], F32)
    ident = cx.sb("aident", [128, 128], BF16)
    identf = cx.sb("aidentf", [128, 128], F32)
    negpad4 = cx.sb("negpad4", [128, 16, 4, 32], F32)
    farind = cx.sb("farind", [128, 16, 32], F32)
    relb31 = cx.sb("relb31", [128, HG], F32)
    B_small = Buf("asmall")
    for dst, src in ((gcol, "gcol"), (ident, "ident"), (identf, "identf"), (negpad4, "negpad"),
                     (farind, "farind"), (relb31, "relb31")):
        cx.dma("sp", dst[:], io[src], writes=[B_small])

    st_es = contextlib.ExitStack()
    stage = [st_es.enter_context(nc.sbuf_tensor("sb_%sastage%d" % (cx.pfx, i), [128, 1536], F32)) for i in range(2)]
    B_stage = bufs("astage", 2)
    B_w = Buf("wqkv")
    for c in range(8):
        st, Bs = stage[c % 2], B_stage[c % 2]
        en = ("act", "dve")[c % 2]
        cx.dma("sp", st[:], io["w_qkv"][c * 128:(c + 1) * 128, :], writes=[Bs])
        if en == "act":
            cx.op(en, lambda e: e.activation(out=w_sb[:, c, :], in_=st[:], func=AF.Copy, scale=gcol[:, c:c + 1]),
                  reads=[Bs, B_small])
        else:
            cx.op(en, lambda e: e.tensor_scalar(out=w_sb[:, c, :], in0=st[:], scalar1=gcol[:, c:c + 1],
                                                scalar2=None, op0=ALU.mult), reads=[Bs, B_small])
        E = cx.eng[en]
        B_w.r.append((E.sem, E.sem.total))
    for en in ("act", "dve", "sp", "pe"):
        cx._wait(cx.eng[en], B_w.r)
    st_es.close()

    x_sb = [cx.sb("ax_sb%d" % i, [128, 4, D], F32) for i in range(2)]
    B_x = bufs("ax", 2)
    xn_bf = cx.sb("axn_bf", [128, D], BF16)
    B_xn = Buf("axn")
    junk = cx.sb("ajunk", [128, D], BF16)
    B_junk = Buf("ajunk")
    ss = cx.sb("ass", [128, 4], F32)
    rstd = cx.sb("arstd", [128, 4], F32)
    B_ss = Buf("ass")
    xnT = cx.sb("axnT", [128, 8, T], BF16)
    B_xnT = Buf("axnT")
    kmT = cx.sb("kmT", [128, HG, 32], F32)
    B_km = Buf("kmT")
    KT_o = [cx.sb("KT_o%d" % i, [128, T], BF16) for i in range(2)]
    B_KTo = bufs("KTo", 2)
    QT_o = [cx.sb("QT_o%d" % i, [128, T], BF16) for i in range(2)]
    B_QTo = bufs("QTo", 2)
    QT_f = [cx.sb("QT_f%d" % i, [128, T], F32) for i in range(2)]
    B_QTf = bufs("QTf", 2)
    gs = cx.sb("gs", [128, 4, 32], F32)
    top8 = cx.sb("top8", [128, 4, 8], F32)
    Mq = cx.sb("Mq", [128, 4, 32], F32)
    B_gs = Buf("gs")
    B_top8 = Buf("top8")
    B_Mq = Buf("Mq")
    MT_o = [cx.sb("MT_o%d" % i, [32, T], BF16) for i in range(2)]
    B_MTo = bufs("MTo", 2)
    V_o = [cx.sb("V_o%d" % i, [128, 4, 512], BF16) for i in range(2)]
    B_Vo = bufs("Vo", 2)

    psT = cx.ps("apsT", [128, 8, 128], BF16)
    B_psT = Buf("apsT", True)
    pqk = [cx.ps("pqk%d" % i, [128, 512], F32) for i in range(3)]
    B_pqk = bufs("pqk", 3, True)
    psG_bank = cx.ps("psG", [128, 512], F32)
    psG = psG_bank[:, 0:32]
    B_psG = Buf("psG", True)
    psMT_bank = cx.ps("psMT", [128, 512], F32)
    psMT = psMT_bank[0:32, 0:128]
    B_psMT = Buf("psMT", True)
    psV = [cx.ps("psV%d" % i, [128, 512], F32) for i in range(2)]
    B_psV = bufs("psV", 2, True)

    cx.op("dve", lambda e: e.memset(kmT[:], 0.0), writes=[B_km])

    def load_x(ti):
        cx.dma("sp", x_sb[ti % 2][:], x[ti * T:(ti + 1) * T, :].rearrange("(s p) d -> p s d", p=128),
               writes=[B_x[ti % 2]])

    load_x(0)
    nq = 0
    nh = 0
    for ti in range(ntile):
        if ti + 1 < ntile:
            load_x(ti + 1)
        xt, Bx = x_sb[ti % 2], B_x[ti % 2]
        t0 = ti * T
        for s in range(4):
            cx.op("act", lambda e: e.activation(out=junk[:], in_=xt[:, s, :], func=AF.Square,
                                                accum_out=ss[:, s:s + 1]), reads=[Bx], writes=[B_junk, B_ss])
        cx.op("dve", lambda e: e.tensor_scalar(out=rstd[:], in0=ss[:], scalar1=1.0 / D, scalar2=EPS,
                                               op0=ALU.mult, op1=ALU.add), reads=[B_ss], writes=[B_ss])
        cx.op("act", lambda e: e.sqrt(out=rstd[:], in_=rstd[:]), reads=[B_ss], writes=[B_ss])
        cx.op("dve", lambda e: e.reciprocal(out=rstd[:], in_=rstd[:]), reads=[B_ss], writes=[B_ss])
        for s in range(4):
            cx.op("act", lambda e: e.activation(out=xn_bf[:], in_=xt[:, s, :], func=AF.Copy,
                                                scale=rstd[:, s:s + 1]), reads=[Bx, B_ss], writes=[B_xn])
            for c in range(8):
                cx.op("pe", lambda e, c=c: e.transpose(psT[:, c, :], xn_bf[:, c * 128:(c + 1) * 128], ident[:]),
                      reads=[B_xn, B_small], writes=[B_psT], inc=(c == 7))
            cx.op("dve", lambda e: e.tensor_copy(out=xnT[:, :, s * 128:(s + 1) * 128], in_=psT[:, :, :]),
                  reads=[B_psT], writes=[B_xnT])
        Vo, BVo = V_o[ti % 2], B_Vo[ti % 2]
        for s in range(4):
            P, Bp = psV[s % 2], B_psV[s % 2]
            for c in range(8):
                cx.op("pe", lambda e, c=c: e.matmul(P[:, :], xnT[:, c, s * 128:(s + 1) * 128], w_sb[:, c, 1024:1536],
                                                   start=(c == 0), stop=(c == 7)),
                      reads=[B_xnT], writes=[Bp], inc=(c == 7))
            if s % 2 == 0:
                cx.op("act", lambda e: e.copy(out=Vo[:, s, :], in_=P[:, :]), reads=[Bp], writes=[BVo])
            else:
                cx.op("dve", lambda e: e.tensor_copy(out=Vo[:, s, :], in_=P[:, :]), reads=[Bp], writes=[BVo])
        for h in range(HG):
            cx.dma("sp", V_d[h, :, 4 * ti:4 * ti + 4, :], Vo[:, :, h * 128:(h + 1) * 128], reads=[BVo])
        def finish_gate(hh, pend):
            Mo, BMo = pend
            for s in range(4):
                cx.op("pe", lambda e: e.transpose(psMT_bank[0:32, s * 128:(s + 1) * 128], Mq[:, s, :], identf[:]),
                      reads=[B_Mq, B_small], writes=[B_psMT], inc=(s == 3))
            cx.op("act", lambda e: e.copy(out=Mo[:, :], in_=psMT_bank[0:32, :]), reads=[B_psMT], writes=[BMo])
            cx.dma("sp", MT_d[hh, :, t0:t0 + T], Mo[:], reads=[BMo])

        pend = None
        for h in range(HG):
            P, Bp = pqk[nq % 3], B_pqk[nq % 3]
            nq += 1
            for c in range(8):
                cx.op("pe", lambda e, c=c: e.matmul(P[:, :], w_sb[:, c, 512 + h * 128:512 + (h + 1) * 128],
                                                   xnT[:, c, :], start=(c == 0), stop=(c == 7)),
                      reads=[B_xnT], writes=[Bp], inc=(c == 7))
            Ko, BKo = KT_o[nh % 2], B_KTo[nh % 2]
            cx.op("act", lambda e: e.copy(out=Ko[:], in_=P[:, :]), reads=[Bp], writes=[BKo])
            cx.op("dve", lambda e: e.tensor_reduce(out=kmT[:, h, 2 * ti:2 * ti + 2],
                                                   in_=P[:, :].rearrange("p (b l) -> p b l", l=256),
                                                   axis=AX.X, op=ALU.add), reads=[Bp], writes=[B_km])
            cx.dma("sp", KT_d[h, :, t0:t0 + T], Ko[:], reads=[BKo])
            P, Bp = pqk[nq % 3], B_pqk[nq % 3]
            nq += 1
            for c in range(8):
                cx.op("pe", lambda e, c=c: e.matmul(P[:, :], w_sb[:, c, h * 128:(h + 1) * 128], xnT[:, c, :],
                                                   start=(c == 0), stop=(c == 7)),
                      reads=[B_xnT], writes=[Bp], inc=(c == 7))
            Qo, BQo = QT_o[nh % 2], B_QTo[nh % 2]
            Qf, BQf = QT_f[nh % 2], B_QTf[nh % 2]
            cx.op("act", lambda e: e.activation(out=Qo[:], in_=P[:, :], func=AF.Copy, scale=128.0 ** -0.5),
                  reads=[Bp], writes=[BQo])
            cx.op("dve", lambda e: e.tensor_copy(out=Qf[:], in_=P[:, :]), reads=[Bp], writes=[BQf])
            cx.dma("sp", QT_d[h, :, t0:t0 + T], Qo[:], reads=[BQo])
            if pend is not None:
                finish_gate(h - 1, pend)
            Mo, BMo = MT_o[nh % 2], B_MTo[nh % 2]
            nh += 1
            for s in range(4):
                cx.op("pe", lambda e: e.matmul(psG_bank[:, s * 32:(s + 1) * 32], Qf[:, s * 128:(s + 1) * 128],
                                               kmT[:, h, :], start=True, stop=True),
                      reads=[BQf, B_km], writes=[B_psG], inc=(s == 3))
            cx.op("dve", lambda e: e.tensor_tensor(out=gs[:, :, :],
                                                  in0=psG_bank[:, 0:128].rearrange("p (s n) -> p s n", n=32),
                                                  in1=negpad4[:, ti, :, :], op=ALU.add),
                  reads=[B_psG, B_small], writes=[B_gs])
            for s in range(4):
                cx.op("dve", lambda e: e.max(out=top8[:, s, :], in_=gs[:, s, :]), reads=[B_gs], writes=[B_top8])
            for s in range(4):
                cx.op("dve", lambda e: e.tensor_scalar(out=gs[:, s, :], in0=gs[:, s, :], scalar1=top8[:, s, 3:4],
                                                       scalar2=NEG, op0=ALU.is_lt, op1=ALU.mult),
                      reads=[B_gs, B_top8], writes=[B_gs])
            for s in range(4):
                cx.op("dve", lambda e: e.scalar_tensor_tensor(out=Mq[:, s, :], in0=farind[:, ti, :],
                                                              scalar=relb31[:, h:h + 1], in1=gs[:, s, :],
                                                              op0=ALU.mult, op1=ALU.add),
                      reads=[B_gs, B_small], writes=[B_Mq])
            pend = (Mo, BMo)
        finish_gate(HG - 1, pend)


def phase_attn_b(cx, io, S_=SEQ):
    nc = cx.nc
    QT_d, KT_d, V_d, MT_d = io["QT_d"], io["KT_d"], io["V_d"], io["MT_d"]
    oT_out = io["oT_out"]
    B_oT = io.get("B_oT")
    T = 512
    ntile = S_ // T
    NCH = S_ // 128

    QT = [cx.sb("QT%d" % i, [128, S_], BF16) for i in range(2)]
    KT = [cx.sb("KT%d" % i, [128, S_], BF16) for i in range(2)]
    Vh = [cx.sb("Vh%d" % i, [128, NCH, 128], BF16) for i in range(2)]
    MT = [cx.sb("MT%d" % i, [32, S_], BF16) for i in range(2)]
    TT = [cx.sb("TT%d" % i, [128, TTW], BF16) for i in range(2)]
    B_hd = bufs("hd", 2)
    Esel = cx.sb("Esel", [32, 32, 128], BF16)
    identb = cx.sb("bident", [128, 128], BF16)
    onesf = cx.sb("onesf", [128, 128], F32)
    B_small = Buf("bsmall")
    cx.dma("sp", Esel[:], io["esel"], writes=[B_small])
    cx.dma("sp", identb[:], io["ident"], writes=[B_small])
    cx.op("dve", lambda e: e.memset(onesf[:], 1.0), writes=[B_small])
    PT = [cx.sb("PT%d" % i, [128, 2, T], BF16) for i in range(3)]
    B_PT = bufs("PT", 3)
    acc = [cx.sb("acc%d" % i, [128, 2, T], F32) for i in range(2)]
    B_acc = bufs("acc", 2)
    rcp = cx.sb("rcp", [128, T], F32)
    B_rcp = Buf("rcp")
    o_sb = [cx.sb("o_sb%d" % i, [128, T], BF16) for i in range(2)]
    B_osb = bufs("osb", 2)

    psS = [cx.ps("psS%d" % i, [128, 2, T], F32) for i in range(2)]
    B_psS = bufs("psS", 2, True)
    psO = [cx.ps("psO%d" % i, [128, T], F32) for i in range(2)]
    B_psO = bufs("psO", 2, True)
    psR = cx.ps("psR", [128, T], F32)
    B_psR = Buf("psR", True)

    def load_head(h):
        i = h % 2
        cx.dma("sp", QT[i][:], QT_d[h], writes=[B_hd[i]])
        cx.dma("sp", KT[i][:], KT_d[h], writes=[B_hd[i]])
        cx.dma("sp", Vh[i][:], V_d[h], writes=[B_hd[i]])
        cx.dma("sp", MT[i][:], MT_d[h], writes=[B_hd[i]])
        cx.dma("sp", TT[i][:], io["tt"][h], writes=[B_hd[i]])

    PD = 1
    units = [(h, jt, kp) for h in range(HG) for jt in range(ntile) for kp in range(2 * jt + 2)]
    tile_no = {}
    for (h, jt, kp) in units:
        tile_no.setdefault((h, jt), len(tile_no))

    def emit_scores(u):
        h, jt, kp = units[u]
        i = h % 2
        Bh = B_hd[i]
        t0 = jt * T
        n = kp
        near = n >= 2 * jt - 4
        S, BS = psS[u % 2], B_psS[u % 2]
        for j in range(2):
            kc = 2 * kp + j
            cx.op("pe", lambda e: e.matmul(S[:, j, :], KT[i][:, kc * 128:(kc + 1) * 128], QT[i][:, t0:t0 + T],
                                           start=True, stop=False), reads=[Bh], writes=[BS], inc=False)
            cx.op("pe", lambda e: e.matmul(S[:, j, :], Esel[:, n, :], MT[i][:, t0:t0 + T],
                                           start=False, stop=(not near)), reads=[Bh, B_small], writes=[BS],
                  inc=(j == 1 and not near))
            if near:
                off = t0 - kc * 128 + 384
                cx.op("pe", lambda e: e.matmul(S[:, j, :], identb[:], TT[i][:, off:off + T],
                                               start=False, stop=True), reads=[Bh, B_small], writes=[BS],
                      inc=(j == 1))

    def emit_rest(u):
        h, jt, kp = units[u]
        if jt == 0 and kp == 0 and h + 1 < HG:
            load_head(h + 1)
        i = h % 2
        Bh = B_hd[i]
        t0 = jt * T
        nt = tile_no[(h, jt)]
        A, BA = acc[nt % 2], B_acc[nt % 2]
        O, BO = psO[nt % 2], B_psO[nt % 2]
        osb, Bos = o_sb[nt % 2], B_osb[nt % 2]
        nkp = 2 * jt + 2
        S, BS = psS[u % 2], B_psS[u % 2]
        P, BP = PT[u % 3], B_PT[u % 3]
        cx.op("act", lambda e: e.activation(out=P[:], in_=S[:, :, :], func=AF.Exp), reads=[BS], writes=[BP])
        if kp == 0:
            cx.op("dve", lambda e: e.tensor_copy(out=A[:], in_=P[:]), reads=[BP], writes=[BA])
        else:
            cx.op("dve", lambda e: e.tensor_tensor(out=A[:], in0=A[:], in1=P[:], op=ALU.add),
                  reads=[BP, BA], writes=[BA])
        for j in range(2):
            kc = 2 * kp + j
            last = (kp == nkp - 1 and j == 1)
            cx.op("pe", lambda e: e.matmul(O[:, :], Vh[i][:, kc, :], P[:, j, :], start=(kc == 0), stop=last),
                  reads=[Bh, BP], writes=[BO], inc=(j == 1))
        if kp == nkp - 1:
            cx.op("pe", lambda e: e.matmul(psR[:, :], onesf[:], A[:, 0, :], start=True, stop=False),
                  reads=[BA, B_small], writes=[B_psR], inc=False)
            cx.op("pe", lambda e: e.matmul(psR[:, :], onesf[:], A[:, 1, :], start=False, stop=True),
                  reads=[BA, B_small], writes=[B_psR])
            cx.op("dve", lambda e: e.reciprocal(out=rcp[:], in_=psR[:, :]), reads=[B_psR], writes=[B_rcp])
            cx.op("dve", lambda e: e.tensor_tensor(out=osb[:], in0=O[:, :], in1=rcp[:], op=ALU.mult),
                  reads=[BO, B_rcp], writes=[Bos])
            cx.dma("sp", oT_out[h * 128:(h + 1) * 128, t0:t0 + T], osb[:], reads=[Bos])

    load_head(0)
    for step in range(len(units) + PD):
        if step < len(units):
            emit_scores(step)
        if step >= PD:
            emit_rest(step - PD)


def attn_consts(S_=SEQ):
    nb = 32
    negpad = np.zeros((32, 32), np.float32)
    for own in range(32):
        negpad[own, own] = 1e30
        negpad[own, own + 1:] = -1e30
    farind = np.zeros((16, 32), np.float32)
    for jt in range(16):
        for n in range(32):
            if n <= 2 * jt - 5:
                farind[jt, n] = 1.0
    esel = np.zeros((32, 32, 128), np.float32)
    for n in range(32):
        esel[n, n, :] = 1.0
    return {
        "negpad": np.ascontiguousarray(np.broadcast_to(
            np.stack([np.stack([negpad[2 * ti + s // 2] for s in range(4)], 0) for ti in range(16)], 0)[None],
            (128, 16, 4, 32))),
        "farind": np.ascontiguousarray(np.broadcast_to(farind[None], (128, 16, 32))),
        "esel": esel.astype(ml_dtypes.bfloat16),
        "ident": np.eye(128, dtype=ml_dtypes.bfloat16),
        "identf": np.eye(128, dtype=np.float32),
    }


def rel_bucket_np(d):
    d = np.maximum(d, 0)
    large = 16 + (np.log(np.maximum(d, 1).astype(np.float32) / 16) / np.log(1024 / 16) * 16).astype(np.int32)
    large = np.minimum(large, 31)
    return np.where(d < 16, d, large)


def attn_tables(rel_bias_g):
    k = np.arange(128)[:, None]
    m = np.arange(TTW)[None, :]
    d = m - k - 384
    idx = rel_bucket_np(d)
    tt = rel_bias_g[:, idx]
    tt = np.where(d[None] >= 0, tt, np.float32(NEG))
    return np.ascontiguousarray(tt).astype(ml_dtypes.bfloat16)


def attn_inputs(x_b, attn_norm, w_qkv, rel_bias, g):
    hs = slice(g * 512, (g + 1) * 512)
    wq = w_qkv[:, 0:1024][:, hs]
    wk = w_qkv[:, 1024:2048][:, hs]
    wv = w_qkv[:, 2048:3072][:, hs]
    m = {"x": x_b, "w_qkv": np.ascontiguousarray(np.concatenate([wq, wk, wv], 1)),
         "gcol": np.ascontiguousarray(attn_norm.reshape(8, 128).T),
         "relb31": np.ascontiguousarray(np.broadcast_to(rel_bias[g * HG:(g + 1) * HG, 31][None, :], (128, HG))),
         "tt": attn_tables(rel_bias[g * HG:(g + 1) * HG])}
    m.update(attn_consts())
    return m


def phase_gdn(cx, io, S_=SEQ):
    nc = cx.nc
    x = io["x"]
    oT_out = io["oT_out"]
    T = 512
    ntile = S_ // T
    NW = 2056

    w_sb = cx.sb("gw_sb", [128, 8, NW], BF16)
    gcol = cx.sb("ggcol", [128, 8], F32)
    convw = cx.sb("gconvw", [128, 12, 4], F32)
    alog = cx.sb("galog", [128, HG], F32)
    negA = cx.sb("gnegA", [128, HG], F32)
    dtb = cx.sb("gdtb", [128, HG], F32)
    onorm = cx.sb("gonorm", [128, 128], F32)
    identb = cx.sb("gident", [128, 128], BF16)
    UTf = cx.sb("gUTf", [128, 128], F32)
    nUTf = cx.sb("gnUTf", [128, 128], F32)
    onesf = cx.sb("gonesf", [128, 128], F32)
    lowneg = cx.sb("glowneg", [128, 128], F32)
    upneg = cx.sb("gupneg", [128, 128], F32)
    maskE = cx.sb("gmaskE", [128, 14, 128], BF16)
    B_small = Buf("gsmall")
    for dst, src in ((gcol, "gcol"), (convw, "convw"), (alog, "alog"), (dtb, "dtb"), (identb, "ident"),
                     (UTf, "UTf"), (nUTf, "nUTf"), (lowneg, "lowneg"), (upneg, "upneg"), (maskE, "maskE")):
        cx.dma("sp", dst[:], io[src], writes=[B_small])
    cx.dma("sp", onorm[:], io["onorm"].partition_broadcast(128), writes=[B_small])
    cx.op("dve", lambda e: e.memset(onesf[:], 1.0), writes=[B_small])
    cx.op("act", lambda e: e.activation(out=negA[:], in_=alog[:], func=AF.Exp), reads=[B_small], writes=[B_small])
    cx.op("dve", lambda e: e.tensor_scalar(out=negA[:], in0=negA[:], scalar1=-1.0, scalar2=None, op0=ALU.mult),
          reads=[B_small], writes=[B_small])
    negA16 = cx.sb("gnegA16", [128, 4 * HG], F32)
    dtb16 = cx.sb("gdtb16", [128, 4 * HG], F32)
    for s4 in range(4):
        cx.op("dve", lambda e: e.tensor_copy(out=negA16[:, s4 * HG:(s4 + 1) * HG], in_=negA[:]),
              reads=[B_small], writes=[B_small])
        cx.op("dve", lambda e: e.tensor_copy(out=dtb16[:, s4 * HG:(s4 + 1) * HG], in_=dtb[:]),
              reads=[B_small], writes=[B_small])

    st_es = contextlib.ExitStack()
    stage = [st_es.enter_context(nc.sbuf_tensor("sb_%sgstage%d" % (cx.pfx, i), [128, NW], F32)) for i in range(2)]
    B_stage = bufs("gstage", 2)
    B_w = Buf("gw")
    for c in range(8):
        st, Bs = stage[c % 2], B_stage[c % 2]
        en = ("act", "dve")[c % 2]
        cx.dma("sp", st[:], io["w_in"][c * 128:(c + 1) * 128, :], writes=[Bs])
        if en == "act":
            cx.op(en, lambda e: e.activation(out=w_sb[:, c, :], in_=st[:], func=AF.Copy, scale=gcol[:, c:c + 1]),
                  reads=[Bs, B_small])
        else:
            cx.op(en, lambda e: e.tensor_scalar(out=w_sb[:, c, :], in0=st[:], scalar1=gcol[:, c:c + 1],
                                                scalar2=None, op0=ALU.mult), reads=[Bs, B_small])
        E = cx.eng[en]
        B_w.r.append((E.sem, E.sem.total))
    for en in ("act", "dve", "sp", "pe"):
        cx._wait(cx.eng[en], B_w.r)
    st_es.close()

    x_sb = [cx.sb("gx_sb%d" % i, [128, 4, D], F32) for i in range(2)]
    B_x = bufs("gx", 2)
    xn_bf = cx.sb("gxn_bf", [128, D], BF16)
    B_xn = Buf("gxn")
    junk = cx.sb("gjunk", [128, D], BF16)
    B_junk = Buf("gjunk")
    ss = cx.sb("gss", [128, 4], F32)
    rstd = cx.sb("grstd", [128, 4], F32)
    B_ss = Buf("gss")
    xnT = cx.sb("gxnT", [128, 8, T], BF16)
    B_xnT = Buf("gxnT")
    cbuf = [cx.sb("gcbuf%d" % i, [128, T + 3], F32) for i in range(2)]
    B_cb = bufs("gcb", 2)
    carry = cx.sb("gcarry", [128, 12, 3], F32)
    B_carry = [Buf("gcarry%d" % j) for j in range(12)]
    cacc = [cx.sb("gcacc%d" % i, [128, T], F32) for i in range(2)]
    B_cacc = bufs("gcacc", 2)
    qkvT = cx.sb("gqkvT", [128, 12, T], BF16)
    B_qkvT = [Buf("gqkvT%d" % j) for j in range(12)]
    zs = cx.sb("gzs", [128, 4, 512], F32)
    B_zs = Buf("gzs")
    ba = cx.sb("gba", [128, 4, 8], F32)
    B_ba = Buf("gba")
    gt = {nm: cx.sb("gt_" + nm, [128, 4 * HG], F32) for nm in
          ("beta", "g", "eG", "eGl", "egl", "nbeG", "t0", "t1")}
    B_gt = Buf("ggt")
    ssq = cx.sb("gssq", [128, 8], F32)
    rqk = cx.sb("grqk", [128, 8], F32)
    rkd = cx.sb("grkd", [128, HG], F32)
    B_ssq = Buf("gssq")
    def mat(nm, dt=BF16, n=2 * HG):
        return [cx.sb("gm_%s%d" % (nm, i), [128, 128], dt) for i in range(n)], bufs("gm_" + nm, n)
    q_n, B_q_n = mat("q_n")
    k_n, B_k_n = mat("k_n")
    k_dec, B_k_dec = mat("k_dec")
    bv, B_bv = mat("bv", F32)
    qT_n, B_qT_n = mat("qT_n")
    kT_n, B_kT_n = mat("kT_n")
    gb, B_gb = mat("gb", F32)
    tmpm, B_tmpm = mat("tmpm", F32)
    Dm, B_Dm = mat("Dm", F32)
    DTm, B_DTm = mat("DTm", F32)
    LkH, B_LkH, UkH, B_UkH, PmH, B_PmH = [], [], [], [], [], []
    for h_ in range(2 * HG):
        for lst, blst, nm_ in ((LkH, B_LkH, "Lk"), (UkH, B_UkH, "Uk"), (PmH, B_PmH, "Pm")):
            m_, b_ = mat("%s_%d_" % (nm_, h_), BF16, 3)
            lst.append(m_)
            blst.append(b_)
    junkh, B_junkh = mat("junkh")
    ssqH = [cx.sb("gssqH%d" % h_, [128, 2], F32) for h_ in range(2 * HG)]
    rqkH = [cx.sb("grqkH%d" % h_, [128, 2], F32) for h_ in range(2 * HG)]
    rkdH = [cx.sb("grkdH%d" % h_, [128, 1], F32) for h_ in range(2 * HG)]
    B_ssqH = bufs("gssqH", 2 * HG)
    osqH = [cx.sb("gosqH%d" % h_, [128, 2], F32) for h_ in range(2 * HG)]
    B_osqH = bufs("gosqH", 2 * HG)
    attnT, B_attnT = mat("attnT")
    Tt, B_Tt = mat("Tt")
    tmpb, B_tmpb = mat("tmpb")
    rr, B_rr = mat("rr")
    vnew, B_vnew = mat("vnew")
    otmp, B_otmp = mat("otmp", F32)
    oh, B_oh = mat("oh", F32)
    oy, B_oy = mat("oy")
    Sf = [cx.sb("gSf%d" % h, [128, 128], F32) for h in range(HG)]
    Sb = [cx.sb("gSb%d" % h, [128, 128], BF16) for h in range(HG)]
    B_S = [Buf("gS%d" % h) for h in range(HG)]
    oT_sb = [cx.sb("goT_sb%d" % i, [128, 128], BF16) for i in range(2 * HG)]
    B_oTsb = bufs("goTsb", 2 * HG)
    osq = cx.sb("gosq", [128, 2], F32)
    B_osq = Buf("gosq")

    psT = cx.ps("gpsT", [128, 8, 128], BF16)
    B_psT = Buf("gpsT", True)
    psg = cx.ps("gpsg", [128, 512], F32)
    B_psg = Buf("gpsg", True)
    NR = 6
    pr = [cx.ps("gpr%d" % i, [128, 512], F32) for i in range(NR)]
    B_pr = bufs("gpr", NR, True)
    nr = [0]

    def ring():
        i = nr[0] % NR
        nr[0] += 1
        return pr[i], B_pr[i]

    free_banks = list(range(NR))

    def rel(st):
        if st["held"] is not None:
            free_banks.append(st["held"])
            st["held"] = None

    def acq(st):
        rel(st)
        while not free_banks:
            yield "WAIT"
        i = free_banks.pop(0)
        st["held"] = i
        return pr[i], B_pr[i]

    for h in range(HG):
        cx.op("dve", lambda e: e.memset(Sf[h][:], 0.0), writes=[B_S[h]])
        cx.op("dve", lambda e: e.memset(Sb[h][:], 0.0), writes=[B_S[h]])
    cx.op("dve", lambda e: e.memset(carry[:], 0.0), writes=B_carry)

    def load_x(ti):
        src = io["x_tile"](ti) if "x_tile" in io else x[ti * T:(ti + 1) * T, :]
        cx.dma("sp", x_sb[ti % 2][:], src.rearrange("(s p) d -> p s d", p=128), writes=[B_x[ti % 2]])

    def mm(out, lhsT, rhs, rd, wr, start=True, stop=True, inc=True):
        cx.op("pe", lambda e: e.matmul(out, lhsT, rhs, start=start, stop=stop), reads=rd, writes=wr, inc=inc)

    load_x(0)
    npj = 0
    ncb = 0
    nm = 0
    for ti in range(ntile):
        if ti + 1 < ntile:
            load_x(ti + 1)
        xt, Bx = x_sb[ti % 2], B_x[ti % 2]
        for s in range(4):
            cx.op("act", lambda e: e.activation(out=junk[:], in_=xt[:, s, :], func=AF.Square,
                                                accum_out=ss[:, s:s + 1]), reads=[Bx], writes=[B_junk, B_ss])
        cx.op("dve", lambda e: e.tensor_scalar(out=rstd[:], in0=ss[:], scalar1=1.0 / D, scalar2=EPS,
                                               op0=ALU.mult, op1=ALU.add), reads=[B_ss], writes=[B_ss])
        cx.op("act", lambda e: e.activation(out=rstd[:], in_=rstd[:], func=AF.Ln), reads=[B_ss], writes=[B_ss])
        cx.op("act", lambda e: e.activation(out=rstd[:], in_=rstd[:], func=AF.Exp, scale=-0.5),
              reads=[B_ss], writes=[B_ss])
        for s in range(4):
            cx.op("act", lambda e: e.activation(out=xn_bf[:], in_=xt[:, s, :], func=AF.Copy,
                                                scale=rstd[:, s:s + 1]), reads=[Bx, B_ss], writes=[B_xn])
            for c in range(8):
                cx.op("pe", lambda e, c=c: e.transpose(psT[:, c, :], xn_bf[:, c * 128:(c + 1) * 128], identb[:]),
                      reads=[B_xn, B_small], writes=[B_psT], inc=(c == 7))
            cx.op("dve", lambda e: e.tensor_copy(out=xnT[:, :, s * 128:(s + 1) * 128], in_=psT[:, :, :]),
                  reads=[B_psT], writes=[B_xnT])
        for j in range(12):
            P, Bp = ring()
            npj += 1
            for c in range(8):
                mm(P[:, :], w_sb[:, c, j * 128:(j + 1) * 128], xnT[:, c, :], [B_xnT], [Bp],
                   start=(c == 0), stop=(c == 7), inc=(c == 7))
            cbf, Bcb = cbuf[ncb % 2], B_cb[ncb % 2]
            ca, Bca = cacc[ncb % 2], B_cacc[ncb % 2]
            ncb += 1
            cx.op("act", lambda e: e.copy(out=cbf[:, 0:3], in_=carry[:, j, :]), reads=[B_carry[j]], writes=[Bcb])
            cx.op("act", lambda e: e.copy(out=cbf[:, 3:3 + T], in_=P[:, :]), reads=[Bp], writes=[Bcb])
            cx.op("act", lambda e: e.copy(out=carry[:, j, :], in_=cbf[:, T:T + 3]), reads=[Bcb],
                  writes=[B_carry[j]])
            cx.op("dve", lambda e: e.tensor_scalar(out=ca[:], in0=cbf[:, 3:3 + T], scalar1=convw[:, j, 3:4],
                                                   scalar2=None, op0=ALU.mult), reads=[Bcb, B_small], writes=[Bca])
            for tap in (2, 1, 0):
                cx.op("dve", lambda e, tap=tap: e.scalar_tensor_tensor(
                    out=ca[:], in0=cbf[:, tap:tap + T], scalar=convw[:, j, tap:tap + 1], in1=ca[:],
                    op0=ALU.mult, op1=ALU.add), reads=[Bcb, Bca, B_small], writes=[Bca])
            cx.op("act", lambda e: e.activation(out=qkvT[:, j, :], in_=ca[:], func=AF.Silu),
                  reads=[Bca], writes=[B_qkvT[j]])
        for s in range(4):
            P, Bp = ring()
            npj += 1
            for c in range(8):
                mm(P[:, :], xnT[:, c, s * 128:(s + 1) * 128], w_sb[:, c, 1536:2048], [B_xnT], [Bp],
                   start=(c == 0), stop=(c == 7), inc=(c == 7))
            cx.op("act", lambda e: e.activation(out=zs[:, s, :], in_=P[:, :], func=AF.Silu),
                  reads=[Bp], writes=[B_zs])
            for c in range(8):
                mm(psg[:, 0:8], xnT[:, c, s * 128:(s + 1) * 128], w_sb[:, c, 2048:2056], [B_xnT], [B_psg],
                   start=(c == 0), stop=(c == 7), inc=(c == 7))
            cx.op("dve", lambda e: e.tensor_copy(out=ba[:, s, :], in_=psg[:, 0:8]), reads=[B_psg], writes=[B_ba])
        G = gt
        cx.op("act", lambda e: e.activation(out=G["beta"][:].rearrange("p (s h) -> p s h", h=HG), in_=ba[:, :, 0:4], func=AF.Exp, scale=-1.0),
              reads=[B_ba], writes=[B_gt])
        cx.op("dve", lambda e: e.tensor_scalar(out=G["beta"][:], in0=G["beta"][:], scalar1=1.0, scalar2=None,
                                               op0=ALU.add), reads=[B_gt], writes=[B_gt])
        cx.op("dve", lambda e: e.reciprocal(out=G["beta"][:], in_=G["beta"][:]), reads=[B_gt], writes=[B_gt])
        cx.op("dve", lambda e: e.tensor_tensor(out=G["t0"][:].rearrange("p (s h) -> p s h", h=HG), in0=ba[:, :, 4:8], in1=dtb16[:].rearrange("p (s h) -> p s h", h=HG), op=ALU.add),
              reads=[B_ba, B_small, B_gt], writes=[B_gt])
        cx.op("act", lambda e: e.activation(out=G["t0"][:], in_=G["t0"][:], func=AF.Exp),
              reads=[B_gt], writes=[B_gt])
        cx.op("dve", lambda e: e.tensor_scalar(out=G["t0"][:], in0=G["t0"][:], scalar1=1.0, scalar2=None,
                                               op0=ALU.add), reads=[B_gt], writes=[B_gt])
        cx.op("act", lambda e: e.activation(out=G["t0"][:], in_=G["t0"][:], func=AF.Ln),
              reads=[B_gt], writes=[B_gt])
        cx.op("dve", lambda e: e.tensor_tensor(out=G["g"][:], in0=G["t0"][:], in1=negA16[:], op=ALU.mult),
              reads=[B_gt, B_small], writes=[B_gt])
        mm(psg[:, 16:32], UTf[:], G["g"][:], [B_gt, B_small], [B_psg])
        mm(psg[:, 32:48], onesf[:], G["g"][:], [B_gt, B_small], [B_psg])
        cx.op("act", lambda e: e.activation(out=G["eG"][:], in_=psg[:, 16:32], func=AF.Exp),
              reads=[B_psg, B_gt], writes=[B_gt])
        cx.op("act", lambda e: e.activation(out=G["egl"][:], in_=psg[:, 32:48], func=AF.Exp),
              reads=[B_psg, B_gt], writes=[B_gt])
        cx.op("dve", lambda e: e.tensor_copy(out=G["t1"][:], in_=psg[:, 16:32]), reads=[B_psg, B_gt],
              writes=[B_gt])
        cx.op("dve", lambda e: e.tensor_tensor(out=G["t1"][:], in0=psg[:, 32:48], in1=G["t1"][:],
                                               op=ALU.subtract), reads=[B_psg, B_gt], writes=[B_gt])
        cx.op("act", lambda e: e.activation(out=G["eGl"][:], in_=G["t1"][:], func=AF.Exp),
              reads=[B_gt], writes=[B_gt])
        cx.op("dve", lambda e: e.scalar_tensor_tensor(out=G["nbeG"][:], in0=G["beta"][:], scalar=-1.0,
                                                      in1=G["eG"][:], op0=ALU.mult, op1=ALU.mult),
              reads=[B_gt], writes=[B_gt])
        for s in range(4):
            ch = ti * 4 + s
            tok = slice(s * 128, (s + 1) * 128)
            G = gt
            def head_gen(h, s, tok, ch):
                i2 = (ch % 2) * HG + h
                i3 = 0
                st = {"held": None}
                Lk, B_Lk, Uk, B_Uk, Pm, B_Pm = LkH[i2], B_LkH[i2], UkH[i2], B_UkH[i2], PmH[i2], B_PmH[i2]
                R, BR = yield from acq(st)
                Rb = R[:, :].bitcast(BF16)
                for w3 in range(3):
                    cx.op("pe", lambda e, w3=w3: e.transpose(Rb[:, w3 * 128:(w3 + 1) * 128],
                                                             qkvT[:, w3 * 4 + h, tok], identb[:]),
                          reads=[B_qkvT[w3 * 4 + h], B_small], writes=[BR], inc=(w3 == 2))
                yield None
                yield cx.op("act", lambda e: e.activation(out=junkh[i2][:], in_=Rb[:, 0:128], func=AF.Square,
                                                    accum_out=ssqH[i2][:, 0:1]), reads=[BR], writes=[B_junkh[i2], B_ssqH[i2]])
                yield cx.op("act", lambda e: e.activation(out=junkh[i2][:], in_=Rb[:, 128:256], func=AF.Square,
                                                    accum_out=ssqH[i2][:, 1:2]), reads=[BR], writes=[B_junkh[i2], B_ssqH[i2]])
                yield cx.op("dve", lambda e: e.tensor_scalar(out=rqkH[i2][:, 0:2], in0=ssqH[i2][:, 0:2], scalar1=EPS, scalar2=None,
                                                       op0=ALU.add), reads=[B_ssqH[i2]], writes=[B_ssqH[i2]])
                yield cx.op("act", lambda e: e.activation(out=rqkH[i2][:, 0:2], in_=rqkH[i2][:, 0:2], func=AF.Ln),
                      reads=[B_ssqH[i2]], writes=[B_ssqH[i2]])
                yield cx.op("act", lambda e: e.activation(out=rqkH[i2][:, 0:2], in_=rqkH[i2][:, 0:2], func=AF.Exp, scale=-0.5),
                      reads=[B_ssqH[i2]], writes=[B_ssqH[i2]])
                yield cx.op("dve", lambda e: e.tensor_scalar(out=rqkH[i2][:, 0:1], in0=rqkH[i2][:, 0:1], scalar1=128.0 ** -0.5,
                                                       scalar2=None, op0=ALU.mult), reads=[B_ssqH[i2]], writes=[B_ssqH[i2]])
                yield cx.op("dve", lambda e: e.tensor_tensor(out=rkdH[i2][:, 0:1], in0=rqkH[i2][:, 1:2], in1=G["eGl"][:, s * HG + h:s * HG + h + 1],
                                                      op=ALU.mult), reads=[B_ssqH[i2], B_gt], writes=[B_ssqH[i2]])
                hc = slice(s * HG + h, s * HG + h + 1)
                yield cx.op("act", lambda e: e.activation(out=q_n[i2][:], in_=Rb[:, 0:128], func=AF.Copy,
                                                    scale=rqkH[i2][:, 0:1]), reads=[BR, B_ssqH[i2]], writes=[B_q_n[i2]])
                yield cx.op("dve", lambda e: e.tensor_scalar(out=k_n[i2][:], in0=Rb[:, 128:256],
                                                       scalar1=rqkH[i2][:, 1:2], scalar2=None, op0=ALU.mult),
                      reads=[BR, B_ssqH[i2]], writes=[B_k_n[i2]])
                yield cx.op("act", lambda e: e.activation(out=k_dec[i2][:], in_=Rb[:, 128:256], func=AF.Copy,
                                                    scale=rkdH[i2][:, 0:1]), reads=[BR, B_ssqH[i2]], writes=[B_k_dec[i2]])
                yield cx.op("dve", lambda e: e.tensor_scalar(out=bv[i2][:], in0=Rb[:, 256:384], scalar1=G["beta"][:, hc],
                                                       scalar2=None, op0=ALU.mult),
                      reads=[BR, B_gt], writes=[B_bv[i2]])
                R2, BR2 = yield from acq(st)
                R2b = R2[:, :].bitcast(BF16)
                cx.op("pe", lambda e: e.transpose(R2b[:, 0:128], q_n[i2][:], identb[:]),
                      reads=[B_q_n[i2], B_small], writes=[BR2], inc=False)
                yield cx.op("pe", lambda e: e.transpose(R2b[:, 128:256], k_n[i2][:], identb[:]),
                      reads=[B_k_n[i2], B_small], writes=[BR2])
                yield cx.op("act", lambda e: e.copy(out=qT_n[i2][:], in_=R2b[:, 0:128]), reads=[BR2], writes=[B_qT_n[i2]])
                yield cx.op("dve", lambda e: e.tensor_copy(out=kT_n[i2][:], in_=R2b[:, 128:256]), reads=[BR2],
                      writes=[B_kT_n[i2]])
                yield cx.op("dve", lambda e: e.tensor_scalar(out=gb[i2][:], in0=onesf[:], scalar1=G["g"][:, hc],
                                                       scalar2=None, op0=ALU.mult),
                      reads=[B_gt, B_small], writes=[B_gb[i2]])
                R3, BR3 = yield from acq(st)
                mm(R3[:, 0:128], UTf[:], gb[i2][:], [B_gb[i2], B_small], [BR3], start=True, stop=False, inc=False)
                yield mm(R3[:, 0:128], gb[i2][:], nUTf[:], [B_gb[i2], B_small], [BR3], start=False, stop=True)
                yield cx.op("dve", lambda e: e.tensor_tensor(out=tmpm[i2][:], in0=R3[:, 0:128], in1=lowneg[:], op=ALU.add),
                      reads=[BR3, B_small], writes=[B_tmpm[i2]])
                yield cx.op("act", lambda e: e.activation(out=Dm[i2][:], in_=tmpm[i2][:], func=AF.Exp),
                      reads=[B_tmpm[i2]], writes=[B_Dm[i2]])
                yield cx.op("dve", lambda e: e.scalar_tensor_tensor(out=tmpm[i2][:], in0=R3[:, 0:128], scalar=-1.0,
                                                              in1=upneg[:], op0=ALU.mult, op1=ALU.add),
                      reads=[BR3, B_small, B_Dm[i2]], writes=[B_tmpm[i2]])
                yield cx.op("act", lambda e: e.activation(out=DTm[i2][:], in_=tmpm[i2][:], func=AF.Exp),
                      reads=[B_tmpm[i2]], writes=[B_DTm[i2]])
                R4, BR4 = yield from acq(st)
                yield mm(R4[:, 0:128], kT_n[i2][:], kT_n[i2][:], [B_kT_n[i2]], [BR4])
                L0, BL0 = Lk[i3], B_Lk[i3]
                yield cx.op("dve", lambda e: e.scalar_tensor_tensor(out=L0[:], in0=R4[:, 0:128], scalar=G["beta"][:, hc],
                                                              in1=Dm[i2][:], op0=ALU.mult, op1=ALU.mult),
                      reads=[BR4, B_gt, B_Dm[i2]], writes=[BL0])
                R5, BR5 = yield from acq(st)
                yield mm(R5[:, 0:128], kT_n[i2][:], qT_n[i2][:], [B_kT_n[i2], B_qT_n[i2]], [BR5])
                yield cx.op("dve", lambda e: e.tensor_tensor(out=attnT[i2][:], in0=R5[:, 0:128], in1=DTm[i2][:],
                                                      op=ALU.mult), reads=[BR5, B_DTm[i2]], writes=[B_attnT[i2]])
                R6, BR6 = yield from acq(st)
                R6b = R6[:, :].bitcast(BF16)
                yield cx.op("pe", lambda e: e.transpose(R6b[:, 0:128], L0[:], identb[:]), reads=[BL0, B_small],
                      writes=[BR6])
                U0, BU0 = Uk[i3], B_Uk[i3]
                yield cx.op("act", lambda e: e.copy(out=U0[:], in_=R6b[:, 0:128]), reads=[BR6], writes=[BU0])
                P0, BP0 = Pm[i3], B_Pm[i3]
                yield cx.op("dve", lambda e: e.tensor_tensor(out=tmpb[i2][:], in0=R6b[:, 0:128], in1=maskE[:, 0, :],
                                                      op=ALU.mult), reads=[BR6, B_small], writes=[B_tmpb[i2]])
                yield cx.op("dve", lambda e: e.tensor_tensor(out=P0[:], in0=identb[:], in1=tmpb[i2][:],
                                                      op=ALU.subtract), reads=[B_tmpb[i2], B_small], writes=[BP0])
                Pc, BPc = P0, BP0
                ia = i3
                for lvl in range(1, 7):
                    ib = (ia + 1) % 3
                    Pn_, BPn = Pm[ib], B_Pm[ib]
                    El, BEl = Uk[lvl % 3], B_Uk[lvl % 3]
                    yield cx.op("dve", lambda e: e.tensor_tensor(out=El[:], in0=L0[:], in1=maskE[:, 7 + lvl, :],
                                                          op=ALU.mult), reads=[BL0, B_small], writes=[BEl])
                    Ra, BRa = yield from acq(st)
                    yield mm(Ra[:, 0:128], El[:], Pc[:], [BEl, BPc], [BRa])
                    Xs, BXs = Lk[(i3 + 1 + lvl % 2) % 3], B_Lk[(i3 + 1 + lvl % 2) % 3]
                    yield cx.op("act", lambda e: e.copy(out=Xs[:], in_=Ra[:, 0:128]), reads=[BRa], writes=[BXs])
                    Rb_, BRb = yield from acq(st)
                    Rbb = Rb_[:, :].bitcast(BF16)
                    yield cx.op("pe", lambda e: e.transpose(Rbb[:, 0:128], Pc[:], identb[:]), reads=[BPc, B_small],
                          writes=[BRb])
                    yield cx.op("act", lambda e: e.copy(out=Tt[i2][:], in_=Rbb[:, 0:128]), reads=[BRb], writes=[B_Tt[i2]])
                    Rc, BRc = yield from acq(st)
                    yield mm(Rc[:, 0:128], Tt[i2][:], Xs[:], [B_Tt[i2], BXs], [BRc])
                    yield cx.op("dve", lambda e: e.tensor_tensor(out=Pn_[:], in0=Pc[:], in1=Rc[:, 0:128],
                                                          op=ALU.subtract), reads=[BRc, BPc], writes=[BPn])
                    Pc, BPc = Pn_, BPn
                    ia = ib
                yield "SPLIT"
                BS_ = B_S[h]
                Rd, BRd = yield from acq(st)
                yield mm(Rd[:, 0:128], kT_n[i2][:], Sb[h][:], [B_kT_n[i2], BS_], [BRd])
                yield cx.op("dve", lambda e: e.scalar_tensor_tensor(out=rr[i2][:], in0=Rd[:, 0:128],
                                                              scalar=G["nbeG"][:, hc], in1=bv[i2][:],
                                                              op0=ALU.mult, op1=ALU.add),
                      reads=[BRd, B_gt, B_bv[i2]], writes=[B_rr[i2]])
                Re, BRe = yield from acq(st)
                yield mm(Re[:, 0:128], Pc[:], rr[i2][:], [BPc, B_rr[i2]], [BRe])
                yield cx.op("act", lambda e: e.copy(out=vnew[i2][:], in_=Re[:, 0:128]), reads=[BRe], writes=[B_vnew[i2]])
                Rf, BRf = yield from acq(st)
                yield mm(Rf[:, 0:128], attnT[i2][:], vnew[i2][:], [B_attnT[i2], B_vnew[i2]], [BRf])
                yield cx.op("act", lambda e: e.copy(out=otmp[i2][:], in_=Rf[:, 0:128]), reads=[BRf], writes=[B_otmp[i2]])
                Rg, BRg = yield from acq(st)
                yield mm(Rg[:, 0:128], qT_n[i2][:], Sb[h][:], [B_qT_n[i2], BS_], [BRg])
                yield cx.op("dve", lambda e: e.scalar_tensor_tensor(out=oh[i2][:], in0=Rg[:, 0:128],
                                                              scalar=G["eG"][:, hc], in1=otmp[i2][:],
                                                              op0=ALU.mult, op1=ALU.add),
                      reads=[BRg, B_gt, B_otmp[i2]], writes=[B_oh[i2]])
                Rh, BRh = yield from acq(st)
                yield mm(Rh[:, 0:128], k_dec[i2][:], vnew[i2][:], [B_k_dec[i2], B_vnew[i2]], [BRh])
                yield cx.op("dve", lambda e: e.scalar_tensor_tensor(out=Sf[h][:], in0=Sf[h][:], scalar=G["egl"][:, hc],
                                                              in1=Rh[:, 0:128], op0=ALU.mult, op1=ALU.add),
                      reads=[BRh, B_gt], writes=[BS_])
                yield cx.op("act", lambda e: e.copy(out=Sb[h][:], in_=Sf[h][:]), reads=[BS_], writes=[BS_])
                yield cx.op("act", lambda e: e.activation(out=junkh[i2][:], in_=oh[i2][:], func=AF.Square,
                                                    accum_out=osqH[i2][:, 0:1]), reads=[B_oh[i2]],
                      writes=[B_junkh[i2], B_osqH[i2]])
                yield cx.op("dve", lambda e: e.tensor_scalar(out=osqH[i2][:, 1:2], in0=osqH[i2][:, 0:1], scalar1=1.0 / 128,
                                                       scalar2=EPS, op0=ALU.mult, op1=ALU.add),
                      reads=[B_osqH[i2]], writes=[B_osqH[i2]])
                yield cx.op("act", lambda e: e.activation(out=osqH[i2][:, 1:2], in_=osqH[i2][:, 1:2], func=AF.Ln),
                      reads=[B_osqH[i2]], writes=[B_osqH[i2]])
                yield cx.op("act", lambda e: e.activation(out=osqH[i2][:, 1:2], in_=osqH[i2][:, 1:2], func=AF.Exp, scale=-0.5),
                      reads=[B_osqH[i2]], writes=[B_osqH[i2]])
                yield cx.op("dve", lambda e: e.scalar_tensor_tensor(out=oh[i2][:], in0=oh[i2][:], scalar=osqH[i2][:, 1:2],
                                                              in1=onorm[:], op0=ALU.mult, op1=ALU.mult),
                      reads=[B_osqH[i2], B_small], writes=[B_oh[i2]])
                yield cx.op("dve", lambda e: e.tensor_tensor(out=oy[i2][:], in0=oh[i2][:],
                                                      in1=zs[:, s, h * 128:(h + 1) * 128], op=ALU.mult),
                      reads=[B_oh[i2], B_zs], writes=[B_oy[i2]])
                Ri, BRi = yield from acq(st)
                Rib = Ri[:, :].bitcast(BF16)
                yield cx.op("pe", lambda e: e.transpose(Rib[:, 0:128], oy[i2][:], identb[:]), reads=[B_oy[i2], B_small],
                      writes=[BRi])
                yield cx.op("act", lambda e: e.copy(out=oT_sb[i2][:], in_=Rib[:, 0:128]), reads=[BRi],
                      writes=[B_oTsb[i2]])
                yield cx.dma("sp", oT_out[h * 128:(h + 1) * 128, ch * 128:(ch + 1) * 128], oT_sb[i2][:],
                       reads=[B_oTsb[i2]])

                rel(st)
            pass

        alive = []
        nxt = 0
        newest = []
        split_seen = 0
        while alive or nxt < 4:
            if nxt < 4 and (not alive or split_seen == HG):
                s_ = nxt
                newest = [head_gen(h, s_, slice(s_ * 128, (s_ + 1) * 128), ti * 4 + s_) for h in range(HG)]
                alive.extend(newest)
                split_seen = 0
                nxt += 1
            for g_ in list(alive):
                try:
                    r_ = next(g_)
                    if r_ == "SPLIT" and g_ in newest:
                        split_seen += 1
                except StopIteration:
                    alive.remove(g_)


def gdn_consts():
    p = np.arange(128)[:, None]
    f = np.arange(128)[None, :]
    UT = (p <= f).astype(np.float32)
    return {
        "UTf": UT, "nUTf": -UT,
        "lowneg": np.where(p > f, 0.0, NEG).astype(np.float32),
        "upneg": np.where(p <= f, 0.0, NEG).astype(np.float32),
        "ident": np.eye(128, dtype=ml_dtypes.bfloat16),
        "maskE": gdn_masks(),
    }


def gdn_masks():
    p = np.arange(128)[:, None]
    f = np.arange(128)[None, :]
    m = np.zeros((128, 14, 128), np.float32)
    for l in range(7):
        b = 2 ** (l + 1)
        same = (p // b) == (f // b)
        low = same & ((p % b) >= b // 2) & ((f % b) < b // 2)
        m[:, 7 + l, :] = low
        m[:, l, :] = low.T
    return m.astype(ml_dtypes.bfloat16)


def gdn_inputs(h_b, gdn_norm, w_in, conv_w, a_log, dt_bias, o_norm, g):
    hs = slice(g * 512, (g + 1) * 512)
    cols = [w_in[:, 0:1024][:, hs], w_in[:, 1024:2048][:, hs], w_in[:, 2048:3072][:, hs],
            w_in[:, 3072:4096][:, hs], w_in[:, 4096 + g * HG:4096 + (g + 1) * HG],
            w_in[:, 4104 + g * HG:4104 + (g + 1) * HG]]
    cw = np.stack([conv_w[:, w3 * 1024:(w3 + 1) * 1024][:, hs] for w3 in range(3)], 0)
    cw = cw.reshape(3, 4, HG, 128).transpose(3, 0, 2, 1).reshape(128, 12, 4)
    m = {"x": h_b, "w_in": np.ascontiguousarray(np.concatenate(cols, 1)),
         "gcol": np.ascontiguousarray(gdn_norm.reshape(8, 128).T),
         "convw": np.ascontiguousarray(cw),
         "alog": np.ascontiguousarray(np.broadcast_to(a_log[g * HG:(g + 1) * HG][None], (128, HG))),
         "dtb": np.ascontiguousarray(np.broadcast_to(dt_bias[g * HG:(g + 1) * HG][None], (128, HG))),
         "onorm": np.ascontiguousarray(o_norm)}
    m.update(gdn_consts())
    return m


def allgather(cx, pairs, groups):
    cx.barrier()
    E = cx.eng["pool"]
    for src, dst in pairs:
        E.e.collective_compute("AllGather", ALU.bypass, replica_groups=groups, ins=[src], outs=[dst]).then_inc(
            cx.ccsem.h, 1)
        cx.ccsem.total += 1
        cx._wait(E, [(cx.ccsem, cx.ccsem.total)])
    for en in cx.eng.values():
        cx._wait(en, [(cx.ccsem, cx.ccsem.total)])


def build_fused(S_=SEQ, n_cores=8, upto=99):
    nc = bass.Bass("TRN2", target_bir_lowering=False)
    H_ = S_ // 2
    groups = [[2 * i, 2 * i + 1] for i in range(n_cores // 2)]

    def din(name, shape, dt=F32):
        return nc.dram_tensor(name, list(shape), dt, kind="ExternalInput").ap()

    def dint(name, shape, dt):
        return nc.dram_tensor(name, list(shape), dt).ap()

    x_full = din("x_full", [S_, D])
    x_half = din("x_half", [HALO + H_, D])
    m01 = din("m01", [128, 2])
    identb = din("identb", [128, 128], BF16)
    io_a = {
        "x": x_full, "w_qkv": din("a_w_qkv", [D, 1536]), "gcol": din("a_gcol", [128, 8]),
        "ident": identb, "identf": din("identf", [128, 128]),
        "negpad": din("negpad", [128, 16, 4, 32]), "farind": din("farind", [128, 16, 32]),
        "relb31": din("relb31", [128, HG]), "esel": din("esel", [32, 32, 128], BF16),
        "tt": din("tt", [HG, 128, TTW], BF16),
        "oT_out": dint("oT_mine1", [HG * 128, S_], BF16),
    }
    io_a.update(attn_scratch(nc, S_))
    oT_full1 = dint("oT_full1", [2 * HG * 128, S_], BF16)
    h1_half = dint("h1_half", [H_, D], F32)
    h1_full = dint("h1_full", [S_, D], F32)
    io_p0 = {
        "hin_main": x_half[HALO:, :], "hin_halo": x_half[0:HALO, :], "oT_full": oT_full1, "half": H_,
        "hout": h1_half, "m01": m01, "ident": identb,
        "w_o": din("p0_w_o", [D, D]), "w_up": din("p0_w_up", [D, 2 * DFF]), "w_down": din("p0_w_down", [DFF, D]),
        "gcol": din("p0_gcol", [128, 8]), "cw": din("p0_cw", [128, NFC, 3]), "cb": din("p0_cb", [128, NFC]),
    }
    nk = H_ // 512
    io_g = {
        "x_tile": lambda ti: h1_full[(ti % nk) * 1024 + (ti // nk) * 512:(ti % nk) * 1024 + (ti // nk) * 512 + 512, :],
        "x": h1_full, "w_in": din("g_w_in", [D, 2056]), "gcol": din("g_gcol", [128, 8]),
        "convw": din("g_convw", [128, 12, 4]), "alog": din("g_alog", [128, HG]), "dtb": din("g_dtb", [128, HG]),
        "onorm": din("g_onorm", [128]), "ident": identb,
        "UTf": din("UTf", [128, 128]), "nUTf": din("nUTf", [128, 128]),
        "lowneg": din("lowneg", [128, 128]), "upneg": din("upneg", [128, 128]),
        "maskE": din("maskE", [128, 14, 128], BF16),
        "oT_out": dint("oT_mine2", [HG * 128, S_], BF16),
    }
    oT_full2 = dint("oT_full2", [2 * HG * 128, S_], BF16)
    io_p1 = {
        "hin_main": h1_half, "hin_halo": h1_full[(nk - 1) * 1024 + 512 - HALO:(nk - 1) * 1024 + 512, :],
        "halo_mask": True, "oT_full": oT_full2,
        "half": H_, "m01": m01, "ident": identb,
        "hout": nc.dram_tensor("out", [H_, D], F32, kind="ExternalOutput").ap(),
        "w_o": din("p1_w_o", [D, D]), "w_up": din("p1_w_up", [D, 2 * DFF]), "w_down": din("p1_w_down", [DFF, D]),
        "gcol": din("p1_gcol", [128, 8]), "cw": din("p1_cw", [128, NFC, 3]), "cb": din("p1_cb", [128, NFC]),
        "gfin": din("gfin", [D]),
    }
    with contextlib.ExitStack() as es:
        cx = Ctx(nc, es)
        cx.ccsem = Sem(es.enter_context(nc.semaphore("ccsem")))

        nph = [0]

        def phase(fn, *a, **kw):
            with contextlib.ExitStack() as es_p:
                cx.es = es_p
                cx.pfx = "p%d_" % nph[0]
                nph[0] += 1
                fn(cx, *a, **kw)
                cx.barrier()

        def ag_heads(mine, full):
            allgather(cx, [(mine[h * 128:(h + 1) * 128, :], full[h * 256:(h + 1) * 256, :]) for h in range(HG)],
                      groups)

        steps = [
            lambda: phase(phase_attn_a, io_a, S_),
            lambda: phase(phase_attn_b, io_a, S_),
            lambda: ag_heads(io_a["oT_out"], oT_full1),
            lambda: phase(phase_post, io_p0, False, n_tok=H_),
            lambda: allgather(cx, [(h1_half[k * 512:(k + 1) * 512, :], h1_full[k * 1024:(k + 1) * 1024, :])
                                   for k in range(nk)], groups),
            lambda: phase(phase_gdn, io_g, S_),
            lambda: ag_heads(io_g["oT_out"], oT_full2),
            lambda: phase(phase_post, io_p1, True, n_tok=H_),
        ]
        for st in steps[:upto]:
            st()
        cx.barrier(final=True)
    return nc


def fused_inputs(x_b, r, rel_bias, attn_norm, attn_w_qkv, attn_w_o, gdn_norm, gdn_w_in, gdn_conv_w, gdn_a_log,
                 gdn_dt_bias, gdn_o_norm, gdn_w_o, ffn_norm, ffn_w_up, ffn_conv_w, ffn_conv_b, ffn_w_down,
                 final_norm):
    S_ = x_b.shape[0]
    H_ = S_ // 2
    m = {"x_full": x_b, "m01": np.ascontiguousarray(np.broadcast_to(np.array([1.0 - r, r], np.float32), (128, 2)))}
    if r == 0:
        m["x_half"] = np.concatenate([np.zeros((HALO, D), np.float32), x_b[:H_]], 0)
    else:
        m["x_half"] = np.ascontiguousarray(x_b[H_ - HALO:])
    a = attn_inputs(x_b, attn_norm, attn_w_qkv, rel_bias, r)
    m.update({"a_w_qkv": a["w_qkv"], "a_gcol": a["gcol"], "identb": a["ident"], "identf": a["identf"],
              "negpad": a["negpad"], "farind": a["farind"], "relb31": a["relb31"], "esel": a["esel"], "tt": a["tt"]})
    p0 = post_small_inputs(ffn_norm[0], ffn_conv_w[0], ffn_conv_b[0])
    perm = np.concatenate([np.arange(128) + ((c % 2) * HG + c // 2) * 128 for c in range(8)])
    attn_w_o = np.ascontiguousarray(attn_w_o[perm])
    gdn_w_o = np.ascontiguousarray(gdn_w_o[perm])
    m.update({"p0_w_o": attn_w_o, "p0_w_up": ffn_w_up[0], "p0_w_down": ffn_w_down[0],
              "p0_gcol": p0["gcol"], "p0_cw": p0["cw"], "p0_cb": p0["cb"]})
    g = gdn_inputs(x_b, gdn_norm, gdn_w_in, gdn_conv_w, gdn_a_log, gdn_dt_bias, gdn_o_norm, r)
    m.update({"g_w_in": g["w_in"], "g_gcol": g["gcol"], "g_convw": g["convw"], "g_alog": g["alog"],
              "g_dtb": g["dtb"], "g_onorm": g["onorm"], "UTf": g["UTf"], "nUTf": g["nUTf"],
              "lowneg": g["lowneg"], "upneg": g["upneg"], "maskE": g["maskE"]})
    p1 = post_small_inputs(ffn_norm[1], ffn_conv_w[1], ffn_conv_b[1])
    m.update({"p1_w_o": gdn_w_o, "p1_w_up": ffn_w_up[1], "p1_w_down": ffn_w_down[1],
              "p1_gcol": p1["gcol"], "p1_cw": p1["cw"], "p1_cb": p1["cb"], "gfin": final_norm})
    return m


_PROG = {}


def kernel(x, rel_bias, attn_norm, attn_w_qkv, attn_w_o, gdn_norm, gdn_w_in, gdn_conv_w, gdn_a_log,
           gdn_dt_bias, gdn_o_norm, gdn_w_o, ffn_norm, ffn_w_up, ffn_conv_w, ffn_conv_b, ffn_w_down, final_norm):
    f32 = lambda a: np.ascontiguousarray(np.asarray(a, dtype=np.float32))
    x = f32(x)
    B, S_ = x.shape[0], x.shape[1]
    n_cores = 2 * B
    key = (S_, n_cores)
    if key not in _PROG:
        _PROG[key] = build_fused(S_, n_cores)
    nc = _PROG[key]
    w = dict(rel_bias=f32(rel_bias), attn_norm=f32(attn_norm[0]), attn_w_qkv=f32(attn_w_qkv[0]),
             attn_w_o=f32(attn_w_o[0]), gdn_norm=f32(gdn_norm[0]), gdn_w_in=f32(gdn_w_in[0]),
             gdn_conv_w=f32(gdn_conv_w[0]), gdn_a_log=f32(gdn_a_log[0]), gdn_dt_bias=f32(gdn_dt_bias[0]),
             gdn_o_norm=f32(gdn_o_norm[0]), gdn_w_o=f32(gdn_w_o[0]), ffn_norm=f32(ffn_norm),
             ffn_w_up=f32(ffn_w_up), ffn_conv_w=f32(ffn_conv_w), ffn_conv_b=f32(ffn_conv_b),
             ffn_w_down=f32(ffn_w_down), final_norm=f32(final_norm))
    maps = [fused_inputs(x[b], r, **w) for b in range(B) for r in range(2)]
    res = run_bass_kernel_spmd(nc, maps, core_ids=list(range(n_cores))).results
    out = np.stack([np.concatenate([np.asarray(res[2 * b + r]["out"]) for r in range(2)], 0) for b in range(B)], 0)
    return out.astype(np.float32)
```

```python
import contextlib
import numpy as np
import ml_dtypes
import concourse.bass as bass
import concourse.mybir as mybir
from concourse.bass_utils import run_bass_kernel_spmd

F32 = mybir.dt.float32
BF16 = mybir.dt.bfloat16
AF = mybir.ActivationFunctionType
ALU = mybir.AluOpType
AX = mybir.AxisListType

D = 1024
DFF = 2816
NFC = DFF // 128
SEQ = 8192
HALF = SEQ // 2
HALO = 128
EPS = 1e-6


class Sem:
    __slots__ = ("h", "total")

    def __init__(self, h):
        self.h = h
        self.total = 0


class Buf:
    __slots__ = ("name", "w", "r", "excl")

    def __init__(self, name="", excl=False):
        self.name = name
        self.w = None
        self.r = []
        self.excl = excl


class Eng:
    def __init__(self, name, e, sem):
        self.name = name
        self.e = e
        self.sem = sem
        self.seen = {}


class Ctx:
    def __init__(self, nc, es, n_dma_sems=12):
        self.nc = nc
        self.es = es
        self.eng = {}
        for name, e in (("pe", nc.tensor), ("act", nc.scalar), ("dve", nc.vector),
                        ("pool", nc.gpsimd), ("sp", nc.sync)):
            self.eng[name] = Eng(name, e, Sem(es.enter_context(nc.semaphore("s_" + name))))
        self.dsem = {}
        self.dpos = {}
        for q in ("sp", "pool", "act"):
            self.dsem[q] = [Sem(es.enter_context(nc.semaphore("d_%s%d" % (q, i)))) for i in range(n_dma_sems)]
            self.dpos[q] = 0
        self.same_engine_sync = True
        self.pfx = ""
        self.embed_waits = True
        self.max_embed = 1

    def _wait(self, E, toks, embed=False):
        need = {}
        for s, v in toks:
            if v > need.get(s, 0):
                need[s] = v
        todo = []
        for s, v in need.items():
            if E.seen.get(s, 0) >= v:
                continue
            if s is E.sem and (E.name == "pe" or not self.same_engine_sync):
                continue
            todo.append((s, v))
        last = []
        if embed and self.embed_waits:
            while todo and len(last) < self.max_embed:
                last.append(todo.pop())
        for s, v in todo:
            E.e.wait_ge(s.h, v)
            E.seen[s] = v
        for s, v in last:
            E.seen[s] = v
        return last

    @staticmethod
    def _deps(reads, writes):
        toks = []
        for b in reads:
            if b.w is not None:
                toks.append(b.w)
        for b in writes:
            if b.w is not None:
                toks.append(b.w)
            toks.extend(b.r)
        return toks

    @staticmethod
    def _mark(tok, reads, writes):
        for b in writes:
            b.w = tok
            b.r = []
        for b in reads:
            b.r.append(tok)

    def op(self, en, fn, reads=(), writes=(), inc=True):
        E = self.eng[en]
        if any(b.excl for b in reads):
            writes = list(writes) + [b for b in reads if b.excl]
            reads = [b for b in reads if not b.excl]
        last = self._wait(E, self._deps(reads, writes), embed=True)
        ins = fn(E.e)
        for s_, v_ in last or ():
            ins._wait_ge(s_.h, v_)
        tok = (E.sem, E.sem.total + 1)
        if inc:
            ins.then_inc(E.sem.h, 1)
            E.sem.total += 1
        self._mark(tok, reads, writes)
        return ins

    def dma(self, q, out, in_, reads=(), writes=(), **kw):
        E = self.eng[q]
        i = self.dpos[q]
        self.dpos[q] = (i + 1) % len(self.dsem[q])
        s = self.dsem[q][i]
        toks = self._deps(reads, writes)
        if s.total > 0:
            toks.append((s, s.total))
        self._wait(E, toks)
        E.e.dma_start(out=out, in_=in_, **kw).then_inc(s.h, 16)
        s.total += 16
        self._mark((s, s.total), reads, writes)

    def barrier(self, final=False):
        toks = []
        for E in self.eng.values():
            if E.sem.total:
                toks.append((E.sem, E.sem.total))
        for q in self.dsem:
            for s in self.dsem[q]:
                if s.total:
                    toks.append((s, s.total))
        names = ("sp",) if final else tuple(self.eng)
        for n in names:
            E = self.eng[n]
            self._wait(E, [t for t in toks if t[0] is not E.sem])

    def sb(self, name, shape, dt):
        return self.es.enter_context(self.nc.sbuf_tensor("sb_" + self.pfx + name, list(shape), dt))

    def ps(self, name, shape, dt):
        return self.es.enter_context(self.nc.psum_tensor("ps_" + self.pfx + name, list(shape), dt))


def bufs(prefix, n, excl=False):
    return [Buf("%s%d" % (prefix, i), excl) for i in range(n)]


def phase_post(cx, io, final, n_tok=HALF, T=256):
    nc = cx.nc
    hin_main, hin_halo, oT_full, hout = io["hin_main"], io["hin_halo"], io["oT_full"], io["hout"]
    halo_mask = io.get("halo_mask", False)
    half = io.get("half", HALF)
    w_o, w_up, w_down = io["w_o"], io["w_up"], io["w_down"]

    wo_sb = cx.sb("wo_sb", [128, 8, D], BF16)
    wup_sb = cx.sb("wup_sb", [128, 8, 2 * DFF], BF16)
    wdn_sb = cx.sb("wdn_sb", [128, NFC, D], BF16)
    gcol = cx.sb("gcol", [128, 8], F32)
    cw = cx.sb("cw", [128, NFC, 3], F32)
    cb = cx.sb("cb", [128, NFC], F32)
    ident = cx.sb("ident", [128, 128], BF16)
    m01 = cx.sb("m01", [128, 2], F32)
    B_small = Buf("small")
    cx.dma("sp", m01[:], io["m01"], writes=[B_small])
    cx.dma("sp", gcol[:], io["gcol"], writes=[B_small])
    cx.dma("sp", cw[:], io["cw"], writes=[B_small])
    cx.dma("sp", cb[:], io["cb"], writes=[B_small])
    cx.dma("sp", ident[:], io["ident"], writes=[B_small])
    if final:
        gfin = cx.sb("gfin", [128, D], F32)
        cx.dma("sp", gfin[:], io["gfin"].partition_broadcast(128), writes=[B_small])

    SW = 1408
    NST = 3
    st_es = contextlib.ExitStack()
    stage = [st_es.enter_context(nc.sbuf_tensor("sb_%sstage%d" % (cx.pfx, i), [128, SW], F32)) for i in range(NST)]
    B_stage = bufs("stage", NST)
    B_wo, B_wup, B_wdn = Buf("wo"), Buf("wup"), Buf("wdn")
    k = 0
    cast_eng = ("act", "dve")

    def load_cast(src_ap, dst_ap, width, scale_ap, Bw):
        nonlocal k
        st, Bs = stage[k % NST], B_stage[k % NST]
        en = cast_eng[k % len(cast_eng)]
        k += 1
        cx.dma("sp", st[:, :width], src_ap, writes=[Bs])
        rd = [Bs, B_small] if scale_ap is not None else [Bs]
        if scale_ap is None:
            if en == "act":
                cx.op(en, lambda e: e.copy(out=dst_ap, in_=st[:, :width]), reads=rd, writes=[])
            else:
                cx.op(en, lambda e: e.tensor_copy(out=dst_ap, in_=st[:, :width]), reads=rd, writes=[])
        else:
            if en == "act":
                cx.op(en, lambda e: e.activation(out=dst_ap, in_=st[:, :width], func=AF.Copy, scale=scale_ap),
                      reads=rd, writes=[])
            else:
                cx.op(en, lambda e: e.tensor_scalar(out=dst_ap, in0=st[:, :width], scalar1=scale_ap, scalar2=None,
                                                    op0=ALU.mult), reads=rd, writes=[])
        E = cx.eng[en]
        Bw.r.append((E.sem, E.sem.total))

    for c in range(8):
        load_cast(w_o[c * 128:(c + 1) * 128, :], wo_sb[:, c, :], D, None, B_wo)
    for c in range(8):
        for j in range(2 * DFF // SW):
            load_cast(w_up[c * 128:(c + 1) * 128, j * SW:(j + 1) * SW], wup_sb[:, c, j * SW:(j + 1) * SW], SW,
                      gcol[:, c:c + 1], B_wup)
    for c in range(NFC):
        load_cast(w_down[c * 128:(c + 1) * 128, :], wdn_sb[:, c, :], D, None, B_wdn)

    def weights_ready(en, Bw):
        cx._wait(cx.eng[en], Bw.r)

    for en in ("act", "dve", "sp"):
        cx._wait(cx.eng[en], B_wo.r + B_wup.r + B_wdn.r)
    st_es.close()

    S = T // 128
    oT_sb = [cx.sb("oT_sb%d" % i, [128, 8, T], BF16) for i in range(2)]
    B_oT = bufs("oT", 2)
    h_sb = [cx.sb("h_sb%d" % i, [128, S, D], F32) for i in range(2)]
    B_h = bufs("h", 2)
    oA = cx.sb("oA", [128, 8, T], BF16)
    B_oA = Buf("oA")
    xn_bf = cx.sb("xn_bf", [128, D], BF16)
    B_xn = Buf("xn")
    junk, B_junk = xn_bf, B_xn
    ss = cx.sb("ss", [128, 8], F32)
    rstd = cx.sb("rstd", [128, 8], F32)
    B_ss = Buf("ss")
    xnT = cx.sb("xnT", [128, 8, T], BF16)
    B_xnT = Buf("xnT")
    gbuf = [cx.sb("gbuf%d" % i, [128, T + 2], F32) for i in range(2)]
    B_g = bufs("g", 2)
    carry = cx.sb("carry", [128, NFC, 2], F32)
    B_carry = [Buf("carry%d" % f) for f in range(NFC)]
    t1 = [cx.sb("t1_%d" % i, [128, T], F32) for i in range(2)]
    B_t1 = bufs("t1", 2)
    t2 = [cx.sb("t2_%d" % i, [128, T], F32) for i in range(2)]
    B_t2 = bufs("t2", 2)
    prodT = cx.sb("prodT", [128, NFC, T], BF16)
    B_prod = Buf("prod")

    pg = [cx.ps("pg%d" % i, [128, 512], F32) for i in range(2)]
    B_pg = bufs("pg", 2, True)
    pv = [cx.ps("pv%d" % i, [128, 512], F32) for i in range(3)]
    B_pv = bufs("pv", 3, True)
    pp = [cx.ps("pp%d" % i, [128, 512], F32) for i in range(2)]
    B_pp = bufs("pp", 2, True)
    psT = cx.ps("psT", [128, 8, 128], BF16)
    B_psT = Buf("psT", True)

    cx.op("dve", lambda e: e.memset(carry[:], 0.0), writes=B_carry)

    tiles = [(0, HALO, True)] + [(HALO + i * T, T, False) for i in range(n_tok // T)]
    npp = 0
    nf = 0

    def load_tile(ti):
        t0, tt, halo = tiles[ti]
        o_t, Bo = oT_sb[ti % 2], B_oT[ti % 2]
        h_t, Bh = h_sb[ti % 2], B_h[ti % 2]
        if halo:
            cx.dma("sp", o_t[:, :, :tt], oT_full[:, half - HALO:half].rearrange("(c p) t -> p c t", p=128),
                   writes=[Bo])
            cx.dma("sp", h_t[:, 0, :], hin_halo, writes=[Bh])
        else:
            p0 = t0 - HALO
            cx.dma("sp", oA[:, :, :tt], oT_full[:, p0:p0 + tt].rearrange("(c p) t -> p c t", p=128), writes=[B_oA])
            cx.dma("sp", o_t[:, :, :tt], oT_full[:, half + p0:half + p0 + tt].rearrange("(c p) t -> p c t", p=128),
                   writes=[Bo])
            cx.dma("sp", h_t[:, :tt // 128, :], hin_main[p0:p0 + tt, :].rearrange("(s p) d -> p s d", p=128),
                   writes=[Bh])

    def blend_tile(ti):
        t0, tt, halo = tiles[ti]
        o_t, Bo = oT_sb[ti % 2], B_oT[ti % 2]
        h_t, Bh = h_sb[ti % 2], B_h[ti % 2]
        if halo:
            cx.op("act", lambda e: e.activation(out=o_t[:, :, :tt], in_=o_t[:, :, :tt], func=AF.Copy,
                                                scale=m01[:, 1:2]), reads=[B_small], writes=[Bo])
            if halo_mask:
                cx.op("dve", lambda e: e.tensor_scalar(out=h_t[:, 0, :], in0=h_t[:, 0, :], scalar1=m01[:, 1:2],
                                                       scalar2=None, op0=ALU.mult), reads=[B_small], writes=[Bh])
        else:
            cx.op("act", lambda e: e.activation(out=oA[:, :, :tt], in_=oA[:, :, :tt], func=AF.Copy,
                                                scale=m01[:, 0:1]), reads=[B_small], writes=[B_oA])
            cx.op("dve", lambda e: e.scalar_tensor_tensor(out=o_t[:, :, :tt], in0=o_t[:, :, :tt],
                                                          scalar=m01[:, 1:2], in1=oA[:, :, :tt],
                                                          op0=ALU.mult, op1=ALU.add),
                  reads=[B_oA, B_small], writes=[Bo])

    def pro1(ti):
        nonlocal npp
        t0, tt, halo = tiles[ti]
        o_t, Bo = oT_sb[ti % 2], B_oT[ti % 2]
        h_t, Bh = h_sb[ti % 2], B_h[ti % 2]
        ns = tt // 128
        for s in range(ns):
            for hf in range(2):
                P, Bp = pp[npp % 2], B_pp[npp % 2]
                npp += 1
                for c in range(8):
                    cx.op("pe", lambda e, c=c: e.matmul(P[:, :], o_t[:, c, s * 128:(s + 1) * 128],
                                                       wo_sb[:, c, hf * 512:(hf + 1) * 512],
                                                       start=(c == 0), stop=(c == 7)),
                          reads=[Bo], writes=[Bp], inc=(c == 7))
                cx.op("dve", lambda e: e.tensor_tensor(out=h_t[:, s, hf * 512:(hf + 1) * 512],
                                                      in0=h_t[:, s, hf * 512:(hf + 1) * 512], in1=P[:, :],
                                                      op=ALU.add), reads=[Bp], writes=[Bh])
        for s in range(ns):
            cx.op("act", lambda e: e.activation(out=junk[:], in_=h_t[:, s, :], func=AF.Square,
                                                accum_out=ss[:, s:s + 1]), reads=[Bh], writes=[B_junk, B_ss])
        cx.op("dve", lambda e: e.tensor_scalar(out=rstd[:, :ns], in0=ss[:, :ns], scalar1=1.0 / D, scalar2=EPS,
                                               op0=ALU.mult, op1=ALU.add), reads=[B_ss], writes=[B_ss])
        cx.op("act", lambda e: e.sqrt(out=rstd[:, :ns], in_=rstd[:, :ns]), reads=[B_ss], writes=[B_ss])
        cx.op("dve", lambda e: e.reciprocal(out=rstd[:, :ns], in_=rstd[:, :ns]), reads=[B_ss], writes=[B_ss])

    def pro2(ti):
        t0, tt, halo = tiles[ti]
        o_t, Bo = oT_sb[ti % 2], B_oT[ti % 2]
        h_t, Bh = h_sb[ti % 2], B_h[ti % 2]
        ns = tt // 128
        for s in range(ns):
            cx.op("act", lambda e: e.activation(out=xn_bf[:], in_=h_t[:, s, :], func=AF.Copy,
                                                scale=rstd[:, s:s + 1]), reads=[Bh, B_ss], writes=[B_xn])
            for c in range(8):
                cx.op("pe", lambda e, c=c: e.transpose(psT[:, c, :], xn_bf[:, c * 128:(c + 1) * 128], ident[:]),
                      reads=[B_xn, B_small], writes=[B_psT], inc=(c == 7))
            cx.op("act", lambda e: e.copy(out=xnT[:, :, s * 128:(s + 1) * 128], in_=psT[:, :, :]),
                  reads=[B_psT], writes=[B_xnT])

    def upproj(ti):
        nonlocal nf
        t0, tt, halo = tiles[ti]
        o_t, Bo = oT_sb[ti % 2], B_oT[ti % 2]
        h_t, Bh = h_sb[ti % 2], B_h[ti % 2]
        ns = tt // 128
        for f in range(NFC):
            G, Bg = pg[nf % 2], B_pg[nf % 2]
            V, Bv = pv[nf % 3], B_pv[nf % 3]
            gb, Bgb = gbuf[nf % 2], B_g[nf % 2]
            a1, Ba1 = t1[nf % 2], B_t1[nf % 2]
            a2, Ba2 = t2[nf % 2], B_t2[nf % 2]
            nf += 1
            for c in range(8):
                cx.op("pe", lambda e, c=c: e.matmul(G[:, :tt], wup_sb[:, c, f * 128:(f + 1) * 128], xnT[:, c, :tt],
                                                   start=(c == 0), stop=(c == 7)),
                      reads=[B_xnT], writes=[Bg], inc=(c == 7))
            if not halo:
                for c in range(8):
                    cx.op("pe", lambda e, c=c: e.matmul(V[:, :tt], wup_sb[:, c, DFF + f * 128:DFF + (f + 1) * 128],
                                                       xnT[:, c, :tt], start=(c == 0), stop=(c == 7)),
                          reads=[B_xnT], writes=[Bv], inc=(c == 7))
            cx.op("act", lambda e: e.copy(out=gb[:, 0:2], in_=carry[:, f, :]),
                  reads=[B_carry[f]], writes=[Bgb])
            cx.op("act", lambda e: e.copy(out=gb[:, 2:2 + tt], in_=G[:, :tt]), reads=[Bg], writes=[Bgb])
            cx.op("act", lambda e: e.copy(out=carry[:, f, :], in_=gb[:, tt:tt + 2]),
                  reads=[Bgb], writes=[B_carry[f]])
            if halo:
                continue
            cx.op("dve", lambda e: e.tensor_scalar(out=a1[:, :tt], in0=gb[:, 2:2 + tt], scalar1=cw[:, f, 2:3],
                                                   scalar2=cb[:, f:f + 1], op0=ALU.mult, op1=ALU.add),
                  reads=[Bgb, B_small], writes=[Ba1])
            cx.op("dve", lambda e: e.scalar_tensor_tensor(out=a1[:, :tt], in0=gb[:, 1:1 + tt], scalar=cw[:, f, 1:2],
                                                          in1=a1[:, :tt], op0=ALU.mult, op1=ALU.add),
                  reads=[Bgb, Ba1], writes=[Ba1])
            cx.op("dve", lambda e: e.scalar_tensor_tensor(out=a1[:, :tt], in0=gb[:, 0:tt], scalar=cw[:, f, 0:1],
                                                          in1=a1[:, :tt], op0=ALU.mult, op1=ALU.add),
                  reads=[Bgb, Ba1], writes=[Ba1])
            cx.op("act", lambda e: e.activation(out=a2[:, :tt], in_=a1[:, :tt], func=AF.Silu),
                  reads=[Ba1], writes=[Ba2])
            cx.op("dve", lambda e: e.tensor_tensor(out=prodT[:, f, :tt], in0=a2[:, :tt], in1=V[:, :tt], op=ALU.mult),
                  reads=[Ba2, Bv], writes=[B_prod])

    def downproj(ti):
        nonlocal npp
        t0, tt, halo = tiles[ti]
        o_t, Bo = oT_sb[ti % 2], B_oT[ti % 2]
        h_t, Bh = h_sb[ti % 2], B_h[ti % 2]
        ns = tt // 128
        for s in range(ns):
            for hf in range(2):
                P, Bp = pp[npp % 2], B_pp[npp % 2]
                npp += 1
                for f in range(NFC):
                    cx.op("pe", lambda e, f=f: e.matmul(P[:, :], prodT[:, f, s * 128:(s + 1) * 128],
                                                       wdn_sb[:, f, hf * 512:(hf + 1) * 512],
                                                       start=(f == 0), stop=(f == NFC - 1)),
                          reads=[B_prod], writes=[Bp], inc=(f == NFC - 1))
                cx.op("dve", lambda e: e.tensor_tensor(out=h_t[:, s, hf * 512:(hf + 1) * 512],
                                                      in0=h_t[:, s, hf * 512:(hf + 1) * 512], in1=P[:, :],
                                                      op=ALU.add), reads=[Bp], writes=[Bh])

    def output(ti):
        t0, tt, halo = tiles[ti]
        o_t, Bo = oT_sb[ti % 2], B_oT[ti % 2]
        h_t, Bh = h_sb[ti % 2], B_h[ti % 2]
        ns = tt // 128
        r0 = t0 - HALO
        if final:
            for s in range(ns):
                cx.op("act", lambda e: e.activation(out=junk[:], in_=h_t[:, s, :], func=AF.Square,
                                                    accum_out=ss[:, 4 + s:5 + s]),
                      reads=[Bh], writes=[B_junk, B_ss])
            cx.op("dve", lambda e: e.tensor_scalar(out=rstd[:, 4:4 + ns], in0=ss[:, 4:4 + ns], scalar1=1.0 / D,
                                                   scalar2=EPS, op0=ALU.mult, op1=ALU.add),
                  reads=[B_ss], writes=[B_ss])
            cx.op("act", lambda e: e.sqrt(out=rstd[:, 4:4 + ns], in_=rstd[:, 4:4 + ns]), reads=[B_ss], writes=[B_ss])
            cx.op("dve", lambda e: e.reciprocal(out=rstd[:, 4:4 + ns], in_=rstd[:, 4:4 + ns]),
                  reads=[B_ss], writes=[B_ss])
            for s in range(ns):
                cx.op("dve", lambda e: e.scalar_tensor_tensor(out=h_t[:, s, :], in0=h_t[:, s, :],
                                                              scalar=rstd[:, 4 + s:5 + s], in1=gfin[:],
                                                              op0=ALU.mult, op1=ALU.mult),
                      reads=[B_ss, B_small], writes=[Bh])
        cx.dma("sp", hout[r0:r0 + tt, :].rearrange("(s p) d -> p s d", p=128), h_t[:, :ns, :], reads=[Bh])

    load_tile(0)
    blend_tile(0)
    weights_ready("pe", B_wo)
    pro1(0)
    pro2(0)
    weights_ready("pe", B_wup)
    for ti in range(len(tiles)):
        halo = tiles[ti][2]
        more = ti + 1 < len(tiles)
        if more:
            load_tile(ti + 1)
        upproj(ti)
        if more:
            blend_tile(ti + 1)
            pro1(ti + 1)
        if not halo:
            if ti == 1:
                weights_ready("pe", B_wdn)
            downproj(ti)
            output(ti)
        if more:
            pro2(ti + 1)


def post_small_inputs(ffn_norm, conv_w, conv_b):
    return {
        "gcol": np.ascontiguousarray(ffn_norm.reshape(8, 128).T),
        "cw": np.ascontiguousarray(conv_w.reshape(3, NFC, 128).transpose(2, 1, 0)),
        "cb": np.ascontiguousarray(conv_b.reshape(NFC, 128).T),
        "ident": np.eye(128, dtype=ml_dtypes.bfloat16),
    }


NEG = -30000.0
HG = 4
TTW = 1920


def attn_scratch(nc, S_):
    def dt_(name, shape, dt):
        return nc.dram_tensor(name, list(shape), dt).ap()
    return {
        "QT_d": dt_("QT_d", [HG, 128, S_], BF16), "KT_d": dt_("KT_d", [HG, 128, S_], BF16),
        "V_d": dt_("V_d", [HG, 128, S_ // 128, 128], BF16), "MT_d": dt_("MT_d", [HG, 32, S_], BF16),
    }


def phase_attn_a(cx, io, S_=SEQ):
    nc = cx.nc
    x = io["x"]
    QT_d, KT_d, V_d, MT_d = io["QT_d"], io["KT_d"], io["V_d"], io["MT_d"]
    T = 512
    ntile = S_ // T

    w_sb = cx.sb("wqkv_sb", [128, 8, 3 * 512], BF16)
    gcol = cx.sb("agcol", [128, 8], F32)
    ident = cx.sb("aident", [128, 128], BF16)
    identf = cx.sb("aidentf", [128, 128], F32)
    negpad4 = cx.sb("negpad4", [128, 16, 4, 32], F32)
    farind = cx.sb("farind", [128, 16, 32], F32)
    relb31 = cx.sb("relb31", [128, HG], F32)
    B_small = Buf("asmall")
    for dst, src in ((gcol, "gcol"), (ident, "ident"), (identf, "identf"), (negpad4, "negpad"),
                     (farind, "farind"), (relb31, "relb31")):
        cx.dma("sp", dst[:], io[src], writes=[B_small])

    st_es = contextlib.ExitStack()
    stage = [st_es.enter_context(nc.sbuf_tensor("sb_%sastage%d" % (cx.pfx, i), [128, 1536], F32)) for i in range(2)]
    B_stage = bufs("astage", 2)
    B_w = Buf("wqkv")
    for c in range(8):
        st, Bs = stage[c % 2], B_stage[c % 2]
        en = ("act", "dve")[c % 2]
        cx.dma("sp", st[:], io["w_qkv"][c * 128:(c + 1) * 128, :], writes=[Bs])
        if en == "act":
            cx.op(en, lambda e: e.activation(out=w_sb[:, c, :], in_=st[:], func=AF.Copy, scale=gcol[:, c:c + 1]),
                  reads=[Bs, B_small])
        else:
            cx.op(en, lambda e: e.tensor_scalar(out=w_sb[:, c, :], in0=st[:], scalar1=gcol[:, c:c + 1],
                                                scalar2=None, op0=ALU.mult), reads=[Bs, B_small])
        E = cx.eng[en]
        B_w.r.append((E.sem, E.sem.total))
    for en in ("act", "dve", "sp", "pe"):
        cx._wait(cx.eng[en], B_w.r)
    st_es.close()

    x_sb = [cx.sb("ax_sb%d" % i, [128, 4, D], F32) for i in range(2)]
    B_x = bufs("ax", 2)
    xn_bf = cx.sb("axn_bf", [128, D], BF16)
    B_xn = Buf("axn")
    junk = cx.sb("ajunk", [128, D], BF16)
    B_junk = Buf("ajunk")
    ss = cx.sb("ass", [128, 4], F32)
    rstd = cx.sb("arstd", [128, 4], F32)
    B_ss = Buf("ass")
    xnT = cx.sb("axnT", [128, 8, T], BF16)
    B_xnT = Buf("axnT")
    kmT = cx.sb("kmT", [128, HG, 32], F32)
    B_km = Buf("kmT")
    KT_o = [cx.sb("KT_o%d" % i, [128, T], BF16) for i in range(2)]
    B_KTo = bufs("KTo", 2)
    QT_o = [cx.sb("QT_o%d" % i, [128, T], BF16) for i in range(2)]
    B_QTo = bufs("QTo", 2)
    QT_f = [cx.sb("QT_f%d" % i, [128, T], F32) for i in range(2)]
    B_QTf = bufs("QTf", 2)
    gs = cx.sb("gs", [128, 4, 32], F32)
    top8 = cx.sb("top8", [128, 4, 8], F32)
    Mq = cx.sb("Mq", [128, 4, 32], F32)
    B_gs = Buf("gs")
    B_top8 = Buf("top8")
    B_Mq = Buf("Mq")
    MT_o = [cx.sb("MT_o%d" % i, [32, T], BF16) for i in range(2)]
    B_MTo = bufs("MTo", 2)
    V_o = [cx.sb("V_o%d" % i, [128, 4, 512], BF16) for i in range(2)]
    B_Vo = bufs("Vo", 2)

    psT = cx.ps("apsT", [128, 8, 128], BF16)
    B_psT = Buf("apsT", True)
    pqk = [cx.ps("pqk%d" % i, [128, 512], F32) for i in range(3)]
    B_pqk = bufs("pqk", 3, True)
    psG_bank = cx.ps("psG", [128, 512], F32)
    psG = psG_bank[:, 0:32]
    B_psG = Buf("psG", True)
    psMT_bank = cx.ps("psMT", [128, 512], F32)
    psMT = psMT_bank[0:32, 0:128]
    B_psMT = Buf("psMT", True)
    psV = [cx.ps("psV%d" % i, [128, 512], F32) for i in range(2)]
    B_psV = bufs("psV", 2, True)

    cx.op("dve", lambda e: e.memset(kmT[:], 0.0), writes=[B_km])

    def load_x(ti):
        cx.dma("sp", x_sb[ti % 2][:], x[ti * T:(ti + 1) * T, :].rearrange("(s p) d -> p s d", p=128),
               writes=[B_x[ti % 2]])

    load_x(0)
    nq = 0
    nh = 0
    for ti in range(ntile):
        if ti + 1 < ntile:
            load_x(ti + 1)
        xt, Bx = x_sb[ti % 2], B_x[ti % 2]
        t0 = ti * T
        for s in range(4):
            cx.op("act", lambda e: e.activation(out=junk[:], in_=xt[:, s, :], func=AF.Square,
                                                accum_out=ss[:, s:s + 1]), reads=[Bx], writes=[B_junk, B_ss])
        cx.op("dve", lambda e: e.tensor_scalar(out=rstd[:], in0=ss[:], scalar1=1.0 / D, scalar2=EPS,
                                               op0=ALU.mult, op1=ALU.add), reads=[B_ss], writes=[B_ss])
        cx.op("act", lambda e: e.sqrt(out=rstd[:], in_=rstd[:]), reads=[B_ss], writes=[B_ss])
        cx.op("dve", lambda e: e.reciprocal(out=rstd[:], in_=rstd[:]), reads=[B_ss], writes=[B_ss])
        for s in range(4):
            cx.op("act", lambda e: e.activation(out=xn_bf[:], in_=xt[:, s, :], func=AF.Copy,
                                                scale=rstd[:, s:s + 1]), reads=[Bx, B_ss], writes=[B_xn])
            for c in range(8):
                cx.op("pe", lambda e, c=c: e.transpose(psT[:, c, :], xn_bf[:, c * 128:(c + 1) * 128], ident[:]),
                      reads=[B_xn, B_small], writes=[B_psT], inc=(c == 7))
            cx.op("dve", lambda e: e.tensor_copy(out=xnT[:, :, s * 128:(s + 1) * 128], in_=psT[:, :, :]),
                  reads=[B_psT], writes=[B_xnT])
        Vo, BVo = V_o[ti % 2], B_Vo[ti % 2]
        for s in range(4):
            P, Bp = psV[s % 2], B_psV[s % 2]
            for c in range(8):
                cx.op("pe", lambda e, c=c: e.matmul(P[:, :], xnT[:, c, s * 128:(s + 1) * 128], w_sb[:, c, 1024:1536],
                                                   start=(c == 0), stop=(c == 7)),
                      reads=[B_xnT], writes=[Bp], inc=(c == 7))
            if s % 2 == 0:
                cx.op("act", lambda e: e.copy(out=Vo[:, s, :], in_=P[:, :]), reads=[Bp], writes=[BVo])
            else:
                cx.op("dve", lambda e: e.tensor_copy(out=Vo[:, s, :], in_=P[:, :]), reads=[Bp], writes=[BVo])
        for h in range(HG):
            cx.dma("sp", V_d[h, :, 4 * ti:4 * ti + 4, :], Vo[:, :, h * 128:(h + 1) * 128], reads=[BVo])
        def finish_gate(hh, pend):
            Mo, BMo = pend
            for s in range(4):
                cx.op("pe", lambda e: e.transpose(psMT_bank[0:32, s * 128:(s + 1) * 128], Mq[:, s, :], identf[:]),
                      reads=[B_Mq, B_small], writes=[B_psMT], inc=(s == 3))
            cx.op("act", lambda e: e.copy(out=Mo[:, :], in_=psMT_bank[0:32, :]), reads=[B_psMT], writes=[BMo])
            cx.dma("sp", MT_d[hh, :, t0:t0 + T], Mo[:], reads=[BMo])

        pend = None
        for h in range(HG):
            P, Bp = pqk[nq % 3], B_pqk[nq % 3]
            nq += 1
            for c in range(8):
                cx.op("pe", lambda e, c=c: e.matmul(P[:, :], w_sb[:, c, 512 + h * 128:512 + (h + 1) * 128],
                                                   xnT[:, c, :], start=(c == 0), stop=(c == 7)),
                      reads=[B_xnT], writes=[Bp], inc=(c == 7))
            Ko, BKo = KT_o[nh % 2], B_KTo[nh % 2]
            cx.op("act", lambda e: e.copy(out=Ko[:], in_=P[:, :]), reads=[Bp], writes=[BKo])
            cx.op("dve", lambda e: e.tensor_reduce(out=kmT[:, h, 2 * ti:2 * ti + 2],
                                                   in_=P[:, :].rearrange("p (b l) -> p b l", l=256),
                                                   axis=AX.X, op=ALU.add), reads=[Bp], writes=[B_km])
            cx.dma("sp", KT_d[h, :, t0:t0 + T], Ko[:], reads=[BKo])
            P, Bp = pqk[nq % 3], B_pqk[nq % 3]
            nq += 1
            for c in range(8):
                cx.op("pe", lambda e, c=c: e.matmul(P[:, :], w_sb[:, c, h * 128:(h + 1) * 128], xnT[:, c, :],
                                                   start=(c == 0), stop=(c == 7)),
                      reads=[B_xnT], writes=[Bp], inc=(c == 7))
            Qo, BQo = QT_o[nh % 2], B_QTo[nh % 2]
            Qf, BQf = QT_f[nh % 2], B_QTf[nh % 2]
            cx.op("act", lambda e: e.activation(out=Qo[:], in_=P[:, :], func=AF.Copy, scale=128.0 ** -0.5),
                  reads=[Bp], writes=[BQo])
            cx.op("dve", lambda e: e.tensor_copy(out=Qf[:], in_=P[:, :]), reads=[Bp], writes=[BQf])
            cx.dma("sp", QT_d[h, :, t0:t0 + T], Qo[:], reads=[BQo])
            if pend is not None:
                finish_gate(h - 1, pend)
            Mo, BMo = MT_o[nh % 2], B_MTo[nh % 2]
            nh += 1
            for s in range(4):
                cx.op("pe", lambda e: e.matmul(psG_bank[:, s * 32:(s + 1) * 32], Qf[:, s * 128:(s + 1) * 128],
                                               kmT[:, h, :], start=True, stop=True),
                      reads=[BQf, B_km], writes=[B_psG], inc=(s == 3))
            cx.op("dve", lambda e: e.tensor_tensor(out=gs[:, :, :],
                                                  in0=psG_bank[:, 0:128].rearrange("p (s n) -> p s n", n=32),
                                                  in1=negpad4[:, ti, :, :], op=ALU.add),
                  reads=[B_psG, B_small], writes=[B_gs])
            for s in range(4):
                cx.op("dve", lambda e: e.max(out=top8[:, s, :], in_=gs[:, s, :]), reads=[B_gs], writes=[B_top8])
            for s in range(4):
                cx.op("dve", lambda e: e.tensor_scalar(out=gs[:, s, :], in0=gs[:, s, :], scalar1=top8[:, s, 3:4],
                                                       scalar2=NEG, op0=ALU.is_lt, op1=ALU.mult),
                      reads=[B_gs, B_top8], writes=[B_gs])
            for s in range(4):
                cx.op("dve", lambda e: e.scalar_tensor_tensor(out=Mq[:, s, :], in0=farind[:, ti, :],
                                                              scalar=relb31[:, h:h + 1], in1=gs[:, s, :],
                                                              op0=ALU.mult, op1=ALU.add),
                      reads=[B_gs, B_small], writes=[B_Mq])
            pend = (Mo, BMo)
        finish_gate(HG - 1, pend)


def phase_attn_b(cx, io, S_=SEQ):
    nc = cx.nc
    QT_d, KT_d, V_d, MT_d = io["QT_d"], io["KT_d"], io["V_d"], io["MT_d"]
    oT_out = io["oT_out"]
    B_oT = io.get("B_oT")
    T = 512
    ntile = S_ // T
    NCH = S_ // 128

    QT = [cx.sb("QT%d" % i, [128, S_], BF16) for i in range(2)]
    KT = [cx.sb("KT%d" % i, [128, S_], BF16) for i in range(2)]
    Vh = [cx.sb("Vh%d" % i, [128, NCH, 128], BF16) for i in range(2)]
    MT = [cx.sb("MT%d" % i, [32, S_], BF16) for i in range(2)]
    TT = [cx.sb("TT%d" % i, [128, TTW], BF16) for i in range(2)]
    B_hd = bufs("hd", 2)
    Esel = cx.sb("Esel", [32, 32, 128], BF16)
    identb = cx.sb("bident", [128, 128], BF16)
    onesf = cx.sb("onesf", [128, 128], F32)
    B_small = Buf("bsmall")
    cx.dma("sp", Esel[:], io["esel"], writes=[B_small])
    cx.dma("sp", identb[:], io["ident"], writes=[B_small])
    cx.op("dve", lambda e: e.memset(onesf[:], 1.0), writes=[B_small])
    PT = [cx.sb("PT%d" % i, [128, 2, T], BF16) for i in range(3)]
    B_PT = bufs("PT", 3)
    acc = [cx.sb("acc%d" % i, [128, 2, T], F32) for i in range(2)]
    B_acc = bufs("acc", 2)
    rcp = cx.sb("rcp", [128, T], F32)
    B_rcp = Buf("rcp")
    o_sb = [cx.sb("o_sb%d" % i, [128, T], BF16) for i in range(2)]
    B_osb = bufs("osb", 2)

    psS = [cx.ps("psS%d" % i, [128, 2, T], F32) for i in range(2)]
    B_psS = bufs("psS", 2, True)
    psO = [cx.ps("psO%d" % i, [128, T], F32) for i in range(2)]
    B_psO = bufs("psO", 2, True)
    psR = cx.ps("psR", [128, T], F32)
    B_psR = Buf("psR", True)

    def load_head(h):
        i = h % 2
        cx.dma("sp", QT[i][:], QT_d[h], writes=[B_hd[i]])
        cx.dma("sp", KT[i][:], KT_d[h], writes=[B_hd[i]])
        cx.dma("sp", Vh[i][:], V_d[h], writes=[B_hd[i]])
        cx.dma("sp", MT[i][:], MT_d[h], writes=[B_hd[i]])
        cx.dma("sp", TT[i][:], io["tt"][h], writes=[B_hd[i]])

    PD = 1
    units = [(h, jt, kp) for h in range(HG) for jt in range(ntile) for kp in range(2 * jt + 2)]
    tile_no = {}
    for (h, jt, kp) in units:
        tile_no.setdefault((h, jt), len(tile_no))

    def emit_scores(u):
        h, jt, kp = units[u]
        i = h % 2
        Bh = B_hd[i]
        t0 = jt * T
        n = kp
        near = n >= 2 * jt - 4
        S, BS = psS[u % 2], B_psS[u % 2]
        for j in range(2):
            kc = 2 * kp + j
            cx.op("pe", lambda e: e.matmul(S[:, j, :], KT[i][:, kc * 128:(kc + 1) * 128], QT[i][:, t0:t0 + T],
                                           start=True, stop=False), reads=[Bh], writes=[BS], inc=False)
            cx.op("pe", lambda e: e.matmul(S[:, j, :], Esel[:, n, :], MT[i][:, t0:t0 + T],
                                           start=False, stop=(not near)), reads=[Bh, B_small], writes=[BS],
                  inc=(j == 1 and not near))
            if near:
                off = t0 - kc * 128 + 384
                cx.op("pe", lambda e: e.matmul(S[:, j, :], identb[:], TT[i][:, off:off + T],
                                               start=False, stop=True), reads=[Bh, B_small], writes=[BS],
                      inc=(j == 1))

    def emit_rest(u):
        h, jt, kp = units[u]
        if jt == 0 and kp == 0 and h + 1 < HG:
            load_head(h + 1)
        i = h % 2
        Bh = B_hd[i]
        t0 = jt * T
        nt = tile_no[(h, jt)]
        A, BA = acc[nt % 2], B_acc[nt % 2]
        O, BO = psO[nt % 2], B_psO[nt % 2]
        osb, Bos = o_sb[nt % 2], B_osb[nt % 2]
        nkp = 2 * jt + 2
        S, BS = psS[u % 2], B_psS[u % 2]
        P, BP = PT[u % 3], B_PT[u % 3]
        cx.op("act", lambda e: e.activation(out=P[:], in_=S[:, :, :], func=AF.Exp), reads=[BS], writes=[BP])
        if kp == 0:
            cx.op("dve", lambda e: e.tensor_copy(out=A[:], in_=P[:]), reads=[BP], writes=[BA])
        else:
            cx.op("dve", lambda e: e.tensor_tensor(out=A[:], in0=A[:], in1=P[:], op=ALU.add),
                  reads=[BP, BA], writes=[BA])
        for j in range(2):
            kc = 2 * kp + j
            last = (kp == nkp - 1 and j == 1)
            cx.op("pe", lambda e: e.matmul(O[:, :], Vh[i][:, kc, :], P[:, j, :], start=(kc == 0), stop=last),
                  reads=[Bh, BP], writes=[BO], inc=(j == 1))
        if kp == nkp - 1:
            cx.op("pe", lambda e: e.matmul(psR[:, :], onesf[:], A[:, 0, :], start=True, stop=False),
                  reads=[BA, B_small], writes=[B_psR], inc=False)
            cx.op("pe", lambda e: e.matmul(psR[:, :], onesf[:], A[:, 1, :], start=False, stop=True),
                  reads=[BA, B_small], writes=[B_psR])
            cx.op("dve", lambda e: e.reciprocal(out=rcp[:], in_=psR[:, :]), reads=[B_psR], writes=[B_rcp])
            cx.op("dve", lambda e: e.tensor_tensor(out=osb[:], in0=O[:, :], in1=rcp[:], op=ALU.mult),
                  reads=[BO, B_rcp], writes=[Bos])
            cx.dma("sp", oT_out[h * 128:(h + 1) * 128, t0:t0 + T], osb[:], reads=[Bos])

    load_head(0)
    for step in range(len(units) + PD):
        if step < len(units):
            emit_scores(step)
        if step >= PD:
            emit_rest(step - PD)


def attn_consts(S_=SEQ):
    nb = 32
    negpad = np.zeros((32, 32), np.float32)
    for own in range(32):
        negpad[own, own] = 1e30
        negpad[own, own + 1:] = -1e30
    farind = np.zeros((16, 32), np.float32)
    for jt in range(16):
        for n in range(32):
            if n <= 2 * jt - 5:
                farind[jt, n] = 1.0
    esel = np.zeros((32, 32, 128), np.float32)
    for n in range(32):
        esel[n, n, :] = 1.0
    return {
        "negpad": np.ascontiguousarray(np.broadcast_to(
            np.stack([np.stack([negpad[2 * ti + s // 2] for s in range(4)], 0) for ti in range(16)], 0)[None],
            (128, 16, 4, 32))),
        "farind": np.ascontiguousarray(np.broadcast_to(farind[None], (128, 16, 32))),
        "esel": esel.astype(ml_dtypes.bfloat16),
        "ident": np.eye(128, dtype=ml_dtypes.bfloat16),
        "identf": np.eye(128, dtype=np.float32),
    }


def rel_bucket_np(d):
    d = np.maximum(d, 0)
    large = 16 + (np.log(np.maximum(d, 1).astype(np.float32) / 16) / np.log(1024 / 16) * 16).astype(np.int32)
    large = np.minimum(large, 31)
    return np.where(d < 16, d, large)


def attn_tables(rel_bias_g):
    k = np.arange(128)[:, None]
    m = np.arange(TTW)[None, :]
    d = m - k - 384
    idx = rel_bucket_np(d)
    tt = rel_bias_g[:, idx]
    tt = np.where(d[None] >= 0, tt, np.float32(NEG))
    return np.ascontiguousarray(tt).astype(ml_dtypes.bfloat16)


def attn_inputs(x_b, attn_norm, w_qkv, rel_bias, g):
    hs = slice(g * 512, (g + 1) * 512)
    wq = w_qkv[:, 0:1024][:, hs]
    wk = w_qkv[:, 1024:2048][:, hs]
    wv = w_qkv[:, 2048:3072][:, hs]
    m = {"x": x_b, "w_qkv": np.ascontiguousarray(np.concatenate([wq, wk, wv], 1)),
         "gcol": np.ascontiguousarray(attn_norm.reshape(8, 128).T),
         "relb31": np.ascontiguousarray(np.broadcast_to(rel_bias[g * HG:(g + 1) * HG, 31][None, :], (128, HG))),
         "tt": attn_tables(rel_bias[g * HG:(g + 1) * HG])}
    m.update(attn_consts())
    return m


def phase_gdn(cx, io, S_=SEQ):
    nc = cx.nc
    x = io["x"]
    oT_out = io["oT_out"]
    T = 512
    ntile = S_ // T
    NW = 2056

    w_sb = cx.sb("gw_sb", [128, 8, NW], BF16)
    gcol = cx.sb("ggcol", [128, 8], F32)
    convw = cx.sb("gconvw", [128, 12, 4], F32)
    alog = cx.sb("galog", [128, HG], F32)
    negA = cx.sb("gnegA", [128, HG], F32)
    dtb = cx.sb("gdtb", [128, HG], F32)
    onorm = cx.sb("gonorm", [128, 128], F32)
    identb = cx.sb("gident", [128, 128], BF16)
    UTf = cx.sb("gUTf", [128, 128], F32)
    nUTf = cx.sb("gnUTf", [128, 128], F32)
    onesf = cx.sb("gonesf", [128, 128], F32)
    lowneg = cx.sb("glowneg", [128, 128], F32)
    upneg = cx.sb("gupneg", [128, 128], F32)
    maskE = cx.sb("gmaskE", [128, 14, 128], BF16)
    B_small = Buf("gsmall")
    for dst, src in ((gcol, "gcol"), (convw, "convw"), (alog, "alog"), (dtb, "dtb"), (identb, "ident"),
                     (UTf, "UTf"), (nUTf, "nUTf"), (lowneg, "lowneg"), (upneg, "upneg"), (maskE, "maskE")):
        cx.dma("sp", dst[:], io[src], writes=[B_small])
    cx.dma("sp", onorm[:], io["onorm"].partition_broadcast(128), writes=[B_small])
    cx.op("dve", lambda e: e.memset(onesf[:], 1.0), writes=[B_small])
    cx.op("act", lambda e: e.activation(out=negA[:], in_=alog[:], func=AF.Exp), reads=[B_small], writes=[B_small])
    cx.op("dve", lambda e: e.tensor_scalar(out=negA[:], in0=negA[:], scalar1=-1.0, scalar2=None, op0=ALU.mult),
          reads=[B_small], writes=[B_small])
    negA16 = cx.sb("gnegA16", [128, 4 * HG], F32)
    dtb16 = cx.sb("gdtb16", [128, 4 * HG], F32)
    for s4 in range(4):
        cx.op("dve", lambda e: e.tensor_copy(out=negA16[:, s4 * HG:(s4 + 1) * HG], in_=negA[:]),
              reads=[B_small], writes=[B_small])
        cx.op("dve", lambda e: e.tensor_copy(out=dtb16[:, s4 * HG:(s4 + 1) * HG], in_=dtb[:]),
              reads=[B_small], writes=[B_small])

    st_es = contextlib.ExitStack()
    stage = [st_es.enter_context(nc.sbuf_tensor("sb_%sgstage%d" % (cx.pfx, i), [128, NW], F32)) for i in range(2)]
    B_stage = bufs("gstage", 2)
    B_w = Buf("gw")
    for c in range(8):
        st, Bs = stage[c % 2], B_stage[c % 2]
        en = ("act", "dve")[c % 2]
        cx.dma("sp", st[:], io["w_in"][c * 128:(c + 1) * 128, :], writes=[Bs])
        if en == "act":
            cx.op(en, lambda e: e.activation(out=w_sb[:, c, :], in_=st[:], func=AF.Copy, scale=gcol[:, c:c + 1]),
                  reads=[Bs, B_small])
        else:
            cx.op(en, lambda e: e.tensor_scalar(out=w_sb[:, c, :], in0=st[:], scalar1=gcol[:, c:c + 1],
                                                scalar2=None, op0=ALU.mult), reads=[Bs, B_small])
        E = cx.eng[en]
        B_w.r.append((E.sem, E.sem.total))
    for en in ("act", "dve", "sp", "pe"):
        cx._wait(cx.eng[en], B_w.r)
    st_es.close()

    x_sb = [cx.sb("gx_sb%d" % i, [128, 4, D], F32) for i in range(2)]
    B_x = bufs("gx", 2)
    xn_bf = cx.sb("gxn_bf", [128, D], BF16)
    B_xn = Buf("gxn")
    junk = cx.sb("gjunk", [128, D], BF16)
    B_junk = Buf("gjunk")
    ss = cx.sb("gss", [128, 4], F32)
    rstd = cx.sb("grstd", [128, 4], F32)
    B_ss = Buf("gss")
    xnT = cx.sb("gxnT", [128, 8, T], BF16)
    B_xnT = Buf("gxnT")
    cbuf = [cx.sb("gcbuf%d" % i, [128, T + 3], F32) for i in range(2)]
    B_cb = bufs("gcb", 2)
    carry = cx.sb("gcarry", [128, 12, 3], F32)
    B_carry = [Buf("gcarry%d" % j) for j in range(12)]
    cacc = [cx.sb("gcacc%d" % i, [128, T], F32) for i in range(2)]
    B_cacc = bufs("gcacc", 2)
    qkvT = cx.sb("gqkvT", [128, 12, T], BF16)
    B_qkvT = [Buf("gqkvT%d" % j) for j in range(12)]
    zs = cx.sb("gzs", [128, 4, 512], F32)
    B_zs = Buf("gzs")
    ba = cx.sb("gba", [128, 4, 8], F32)
    B_ba = Buf("gba")
    gt = {nm: cx.sb("gt_" + nm, [128, 4 * HG], F32) for nm in
          ("beta", "g", "eG", "eGl", "egl", "nbeG", "t0", "t1")}
    B_gt = Buf("ggt")
    ssq = cx.sb("gssq", [128, 8], F32)
    rqk = cx.sb("grqk", [128, 8], F32)
    rkd = cx.sb("grkd", [128, HG], F32)
    B_ssq = Buf("gssq")
    def mat(nm, dt=BF16, n=2 * HG):
        return [cx.sb("gm_%s%d" % (nm, i), [128, 128], dt) for i in range(n)], bufs("gm_" + nm, n)
    q_n, B_q_n = mat("q_n")
    k_n, B_k_n = mat("k_n")
    k_dec, B_k_dec = mat("k_dec")
    bv, B_bv = mat("bv", F32)
    qT_n, B_qT_n = mat("qT_n")
    kT_n, B_kT_n = mat("kT_n")
    gb, B_gb = mat("gb", F32)
    tmpm, B_tmpm = mat("tmpm", F32)
    Dm, B_Dm = mat("Dm", F32)
    DTm, B_DTm = mat("DTm", F32)
    LkH, B_LkH, UkH, B_UkH, PmH, B_PmH = [], [], [], [], [], []
    for h_ in range(2 * HG):
        for lst, blst, nm_ in ((LkH, B_LkH, "Lk"), (UkH, B_UkH, "Uk"), (PmH, B_PmH, "Pm")):
            m_, b_ = mat("%s_%d_" % (nm_, h_), BF16, 3)
            lst.append(m_)
            blst.append(b_)
    junkh, B_junkh = mat("junkh")
    ssqH = [cx.sb("gssqH%d" % h_, [128, 2], F32) for h_ in range(2 * HG)]
    rqkH = [cx.sb("grqkH%d" % h_, [128, 2], F32) for h_ in range(2 * HG)]
    rkdH = [cx.sb("grkdH%d" % h_, [128, 1], F32) for h_ in range(2 * HG)]
    B_ssqH = bufs("gssqH", 2 * HG)
    osqH = [cx.sb("gosqH%d" % h_, [128, 2], F32) for h_ in range(2 * HG)]
    B_osqH = bufs("gosqH", 2 * HG)
    attnT, B_attnT = mat("attnT")
    Tt, B_Tt = mat("Tt")
    tmpb, B_tmpb = mat("tmpb")
    rr, B_rr = mat("rr")
    vnew, B_vnew = mat("vnew")
    otmp, B_otmp = mat("otmp", F32)
    oh, B_oh = mat("oh", F32)
    oy, B_oy = mat("oy")
    Sf = [cx.sb("gSf%d" % h, [128, 128], F32) for h in range(HG)]
    Sb = [cx.sb("gSb%d" % h, [128, 128], BF16) for h in range(HG)]
    B_S = [Buf("gS%d" % h) for h in range(HG)]
    oT_sb = [cx.sb("goT_sb%d" % i, [128, 128], BF16) for i in range(2 * HG)]
    B_oTsb = bufs("goTsb", 2 * HG)
    osq = cx.sb("gosq", [128, 2], F32)
    B_osq = Buf("gosq")

    psT = cx.ps("gpsT", [128, 8, 128], BF16)
    B_psT = Buf("gpsT", True)
    psg = cx.ps("gpsg", [128, 512], F32)
    B_psg = Buf("gpsg", True)
    NR = 6
    pr = [cx.ps("gpr%d" % i, [128, 512], F32) for i in range(NR)]
    B_pr = bufs("gpr", NR, True)
    nr = [0]

    def ring():
        i = nr[0] % NR
        nr[0] += 1
        return pr[i], B_pr[i]

    free_banks = list(range(NR))

    def rel(st):
        if st["held"] is not None:
            free_banks.append(st["held"])
            st["held"] = None

    def acq(st):
        rel(st)
        while not free_banks:
            yield "WAIT"
        i = free_banks.pop(0)
        st["held"] = i
        return pr[i], B_pr[i]

    for h in range(HG):
        cx.op("dve", lambda e: e.memset(Sf[h][:], 0.0), writes=[B_S[h]])
        cx.op("dve", lambda e: e.memset(Sb[h][:], 0.0), writes=[B_S[h]])
    cx.op("dve", lambda e: e.memset(carry[:], 0.0), writes=B_carry)

    def load_x(ti):
        src = io["x_tile"](ti) if "x_tile" in io else x[ti * T:(ti + 1) * T, :]
        cx.dma("sp", x_sb[ti % 2][:], src.rearrange("(s p) d -> p s d", p=128), writes=[B_x[ti % 2]])

    def mm(out, lhsT, rhs, rd, wr, start=True, stop=True, inc=True):
        cx.op("pe", lambda e: e.matmul(out, lhsT, rhs, start=start, stop=stop), reads=rd, writes=wr, inc=inc)

    load_x(0)
    npj = 0
    ncb = 0
    nm = 0
    for ti in range(ntile):
        if ti + 1 < ntile:
            load_x(ti + 1)
        xt, Bx = x_sb[ti % 2], B_x[ti % 2]
        for s in range(4):
            cx.op("act", lambda e: e.activation(out=junk[:], in_=xt[:, s, :], func=AF.Square,
                                                accum_out=ss[:, s:s + 1]), reads=[Bx], writes=[B_junk, B_ss])
        cx.op("dve", lambda e: e.tensor_scalar(out=rstd[:], in0=ss[:], scalar1=1.0 / D, scalar2=EPS,
                                               op0=ALU.mult, op1=ALU.add), reads=[B_ss], writes=[B_ss])
        cx.op("act", lambda e: e.activation(out=rstd[:], in_=rstd[:], func=AF.Ln), reads=[B_ss], writes=[B_ss])
        cx.op("act", lambda e: e.activation(out=rstd[:], in_=rstd[:], func=AF.Exp, scale=-0.5),
              reads=[B_ss], writes=[B_ss])
        for s in range(4):
            cx.op("act", lambda e: e.activation(out=xn_bf[:], in_=xt[:, s, :], func=AF.Copy,
                                                scale=rstd[:, s:s + 1]), reads=[Bx, B_ss], writes=[B_xn])
            for c in range(8):
                cx.op("pe", lambda e, c=c: e.transpose(psT[:, c, :], xn_bf[:, c * 128:(c + 1) * 128], identb[:]),
                      reads=[B_xn, B_small], writes=[B_psT], inc=(c == 7))
            cx.op("dve", lambda e: e.tensor_copy(out=xnT[:, :, s * 128:(s + 1) * 128], in_=psT[:, :, :]),
                  reads=[B_psT], writes=[B_xnT])
        for j in range(12):
            P, Bp = ring()
            npj += 1
            for c in range(8):
                mm(P[:, :], w_sb[:, c, j * 128:(j + 1) * 128], xnT[:, c, :], [B_xnT], [Bp],
                   start=(c == 0), stop=(c == 7), inc=(c == 7))
            cbf, Bcb = cbuf[ncb % 2], B_cb[ncb % 2]
            ca, Bca = cacc[ncb % 2], B_cacc[ncb % 2]
            ncb += 1
            cx.op("act", lambda e: e.copy(out=cbf[:, 0:3], in_=carry[:, j, :]), reads=[B_carry[j]], writes=[Bcb])
            cx.op("act", lambda e: e.copy(out=cbf[:, 3:3 + T], in_=P[:, :]), reads=[Bp], writes=[Bcb])
            cx.op("act", lambda e: e.copy(out=carry[:, j, :], in_=cbf[:, T:T + 3]), reads=[Bcb],
                  writes=[B_carry[j]])
            cx.op("dve", lambda e: e.tensor_scalar(out=ca[:], in0=cbf[:, 3:3 + T], scalar1=convw[:, j, 3:4],
                                                   scalar2=None, op0=ALU.mult), reads=[Bcb, B_small], writes=[Bca])
            for tap in (2, 1, 0):
                cx.op("dve", lambda e, tap=tap: e.scalar_tensor_tensor(
                    out=ca[:], in0=cbf[:, tap:tap + T], scalar=convw[:, j, tap:tap + 1], in1=ca[:],
                    op0=ALU.mult, op1=ALU.add), reads=[Bcb, Bca, B_small], writes=[Bca])
            cx.op("act", lambda e: e.activation(out=qkvT[:, j, :], in_=ca[:], func=AF.Silu),
                  reads=[Bca], writes=[B_qkvT[j]])
        for s in range(4):
            P, Bp = ring()
            npj += 1
            for c in range(8):
                mm(P[:, :], xnT[:, c, s * 128:(s + 1) * 128], w_sb[:, c, 1536:2048], [B_xnT], [Bp],
                   start=(c == 0), stop=(c == 7), inc=(c == 7))
            cx.op("act", lambda e: e.activation(out=zs[:, s, :], in_=P[:, :], func=AF.Silu),
                  reads=[Bp], writes=[B_zs])
            for c in range(8):
                mm(psg[:, 0:8], xnT[:, c, s * 128:(s + 1) * 128], w_sb[:, c, 2048:2056], [B_xnT], [B_psg],
                   start=(c == 0), stop=(c == 7), inc=(c == 7))
            cx.op("dve", lambda e: e.tensor_copy(out=ba[:, s, :], in_=psg[:, 0:8]), reads=[B_psg], writes=[B_ba])
        G = gt
        cx.op("act", lambda e: e.activation(out=G["beta"][:].rearrange("p (s h) -> p s h", h=HG), in_=ba[:, :, 0:4], func=AF.Exp, scale=-1.0),
              reads=[B_ba], writes=[B_gt])
        cx.op("dve", lambda e: e.tensor_scalar(out=G["beta"][:], in0=G["beta"][:], scalar1=1.0, scalar2=None,
                                               op0=ALU.add), reads=[B_gt], writes=[B_gt])
        cx.op("dve", lambda e: e.reciprocal(out=G["beta"][:], in_=G["beta"][:]), reads=[B_gt], writes=[B_gt])
        cx.op("dve", lambda e: e.tensor_tensor(out=G["t0"][:].rearrange("p (s h) -> p s h", h=HG), in0=ba[:, :, 4:8], in1=dtb16[:].rearrange("p (s h) -> p s h", h=HG), op=ALU.add),
              reads=[B_ba, B_small, B_gt], writes=[B_gt])
        cx.op("act", lambda e: e.activation(out=G["t0"][:], in_=G["t0"][:], func=AF.Exp),
              reads=[B_gt], writes=[B_gt])
        cx.op("dve", lambda e: e.tensor_scalar(out=G["t0"][:], in0=G["t0"][:], scalar1=1.0, scalar2=None,
                                               op0=ALU.add), reads=[B_gt], writes=[B_gt])
        cx.op("act", lambda e: e.activation(out=G["t0"][:], in_=G["t0"][:], func=AF.Ln),
              reads=[B_gt], writes=[B_gt])
        cx.op("dve", lambda e: e.tensor_tensor(out=G["g"][:], in0=G["t0"][:], in1=negA16[:], op=ALU.mult),
              reads=[B_gt, B_small], writes=[B_gt])
        mm(psg[:, 16:32], UTf[:], G["g"][:], [B_gt, B_small], [B_psg])
        mm(psg[:, 32:48], onesf[:], G["g"][:], [B_gt, B_small], [B_psg])
        cx.op("act", lambda e: e.activation(out=G["eG"][:], in_=psg[:, 16:32], func=AF.Exp),
              reads=[B_psg, B_gt], writes=[B_gt])
        cx.op("act", lambda e: e.activation(out=G["egl"][:], in_=psg[:, 32:48], func=AF.Exp),
              reads=[B_psg, B_gt], writes=[B_gt])
        cx.op("dve", lambda e: e.tensor_copy(out=G["t1"][:], in_=psg[:, 16:32]), reads=[B_psg, B_gt],
              writes=[B_gt])
        cx.op("dve", lambda e: e.tensor_tensor(out=G["t1"][:], in0=psg[:, 32:48], in1=G["t1"][:],
                                               op=ALU.subtract), reads=[B_psg, B_gt], writes=[B_gt])
        cx.op("act", lambda e: e.activation(out=G["eGl"][:], in_=G["t1"][:], func=AF.Exp),
              reads=[B_gt], writes=[B_gt])
        cx.op("dve", lambda e: e.scalar_tensor_tensor(out=G["nbeG"][:], in0=G["beta"][:], scalar=-1.0,
                                                      in1=G["eG"][:], op0=ALU.mult, op1=ALU.mult),
              reads=[B_gt], writes=[B_gt])
        for s in range(4):
            ch = ti * 4 + s
            tok = slice(s * 128, (s + 1) * 128)
            G = gt
            def head_gen(h, s, tok, ch):
                i2 = (ch % 2) * HG + h
                i3 = 0
                st = {"held": None}
                Lk, B_Lk, Uk, B_Uk, Pm, B_Pm = LkH[i2], B_LkH[i2], UkH[i2], B_UkH[i2], PmH[i2], B_PmH[i2]
                R, BR = yield from acq(st)
                Rb = R[:, :].bitcast(BF16)
                for w3 in range(3):
                    cx.op("pe", lambda e, w3=w3: e.transpose(Rb[:, w3 * 128:(w3 + 1) * 128],
                                                             qkvT[:, w3 * 4 + h, tok], identb[:]),
                          reads=[B_qkvT[w3 * 4 + h], B_small], writes=[BR], inc=(w3 == 2))
                yield None
                yield cx.op("act", lambda e: e.activation(out=junkh[i2][:], in_=Rb[:, 0:128], func=AF.Square,
                                                    accum_out=ssqH[i2][:, 0:1]), reads=[BR], writes=[B_junkh[i2], B_ssqH[i2]])
                yield cx.op("act", lambda e: e.activation(out=junkh[i2][:], in_=Rb[:, 128:256], func=AF.Square,
                                                    accum_out=ssqH[i2][:, 1:2]), reads=[BR], writes=[B_junkh[i2], B_ssqH[i2]])
                yield cx.op("dve", lambda e: e.tensor_scalar(out=rqkH[i2][:, 0:2], in0=ssqH[i2][:, 0:2], scalar1=EPS, scalar2=None,
                                                       op0=ALU.add), reads=[B_ssqH[i2]], writes=[B_ssqH[i2]])
                yield cx.op("act", lambda e: e.activation(out=rqkH[i2][:, 0:2], in_=rqkH[i2][:, 0:2], func=AF.Ln),
                      reads=[B_ssqH[i2]], writes=[B_ssqH[i2]])
                yield cx.op("act", lambda e: e.activation(out=rqkH[i2][:, 0:2], in_=rqkH[i2][:, 0:2], func=AF.Exp, scale=-0.5),
                      reads=[B_ssqH[i2]], writes=[B_ssqH[i2]])
                yield cx.op("dve", lambda e: e.tensor_scalar(out=rqkH[i2][:, 0:1], in0=rqkH[i2][:, 0:1], scalar1=128.0 ** -0.5,
                                                       scalar2=None, op0=ALU.mult), reads=[B_ssqH[i2]], writes=[B_ssqH[i2]])
                yield cx.op("dve", lambda e: e.tensor_tensor(out=rkdH[i2][:, 0:1], in0=rqkH[i2][:, 1:2], in1=G["eGl"][:, s * HG + h:s * HG + h + 1],
                                                      op=ALU.mult), reads=[B_ssqH[i2], B_gt], writes=[B_ssqH[i2]])
                hc = slice(s * HG + h, s * HG + h + 1)
                yield cx.op("act", lambda e: e.activation(out=q_n[i2][:], in_=Rb[:, 0:128], func=AF.Copy,
                                                    scale=rqkH[i2][:, 0:1]), reads=[BR, B_ssqH[i2]], writes=[B_q_n[i2]])
                yield cx.op("dve", lambda e: e.tensor_scalar(out=k_n[i2][:], in0=Rb[:, 128:256],
                                                       scalar1=rqkH[i2][:, 1:2], scalar2=None, op0=ALU.mult),
                      reads=[BR, B_ssqH[i2]], writes=[B_k_n[i2]])
                yield cx.op("act", lambda e: e.activation(out=k_dec[i2][:], in_=Rb[:, 128:256], func=AF.Copy,
                                                    scale=rkdH[i2][:, 0:1]), reads=[BR, B_ssqH[i2]], writes=[B_k_dec[i2]])
                yield cx.op("dve", lambda e: e.tensor_scalar(out=bv[i2][:], in0=Rb[:, 256:384], scalar1=G["beta"][:, hc],
                                                       scalar2=None, op0=ALU.mult),
                      reads=[BR, B_gt], writes=[B_bv[i2]])
                R2, BR2 = yield from acq(st)
                R2b = R2[:, :].bitcast(BF16)
                cx.op("pe", lambda e: e.transpose(R2b[:, 0:128], q_n[i2][:], identb[:]),
                      reads=[B_q_n[i2], B_small], writes=[BR2], inc=False)
                yield cx.op("pe", lambda e: e.transpose(R2b[:, 128:256], k_n[i2][:], identb[:]),
                      reads=[B_k_n[i2], B_small], writes=[BR2])
                yield cx.op("act", lambda e: e.copy(out=qT_n[i2][:], in_=R2b[:, 0:128]), reads=[BR2], writes=[B_qT_n[i2]])
                yield cx.op("dve", lambda e: e.tensor_copy(out=kT_n[i2][:], in_=R2b[:, 128:256]), reads=[BR2],
                      writes=[B_kT_n[i2]])
                yield cx.op("dve", lambda e: e.tensor_scalar(out=gb[i2][:], in0=onesf[:], scalar1=G["g"][:, hc],
                                                       scalar2=None, op0=ALU.mult),
                      reads=[B_gt, B_small], writes=[B_gb[i2]])
                R3, BR3 = yield from acq(st)
                mm(R3[:, 0:128], UTf[:], gb[i2][:], [B_gb[i2], B_small], [BR3], start=True, stop=False, inc=False)
                yield mm(R3[:, 0:128], gb[i2][:], nUTf[:], [B_gb[i2], B_small], [BR3], start=False, stop=True)
                yield cx.op("dve", lambda e: e.tensor_tensor(out=tmpm[i2][:], in0=R3[:, 0:128], in1=lowneg[:], op=ALU.add),
                      reads=[BR3, B_small], writes=[B_tmpm[i2]])
                yield cx.op("act", lambda e: e.activation(out=Dm[i2][:], in_=tmpm[i2][:], func=AF.Exp),
                      reads=[B_tmpm[i2]], writes=[B_Dm[i2]])
                yield cx.op("dve", lambda e: e.scalar_tensor_tensor(out=tmpm[i2][:], in0=R3[:, 0:128], scalar=-1.0,
                                                              in1=upneg[:], op0=ALU.mult, op1=ALU.add),
                      reads=[BR3, B_small, B_Dm[i2]], writes=[B_tmpm[i2]])
                yield cx.op("act", lambda e: e.activation(out=DTm[i2][:], in_=tmpm[i2][:], func=AF.Exp),
                      reads=[B_tmpm[i2]], writes=[B_DTm[i2]])
                R4, BR4 = yield from acq(st)
                yield mm(R4[:, 0:128], kT_n[i2][:], kT_n[i2][:], [B_kT_n[i2]], [BR4])
                L0, BL0 = Lk[i3], B_Lk[i3]
                yield cx.op("dve", lambda e: e.scalar_tensor_tensor(out=L0[:], in0=R4[:, 0:128], scalar=G["beta"][:, hc],
                                                              in1=Dm[i2][:], op0=ALU.mult, op1=ALU.mult),
                      reads=[BR4, B_gt, B_Dm[i2]], writes=[BL0])
                R5, BR5 = yield from acq(st)
                yield mm(R5[:, 0:128], kT_n[i2][:], qT_n[i2][:], [B_kT_n[i2], B_qT_n[i2]], [BR5])
                yield cx.op("dve", lambda e: e.tensor_tensor(out=attnT[i2][:], in0=R5[:, 0:128], in1=DTm[i2][:],
                                                      op=ALU.mult), reads=[BR5, B_DTm[i2]], writes=[B_attnT[i2]])
                R6, BR6 = yield from acq(st)
                R6b = R6[:, :].bitcast(BF16)
                yield cx.op("pe", lambda e: e.transpose(R6b[:, 0:128], L0[:], identb[:]), reads=[BL0, B_small],
                      writes=[BR6])
                U0, BU0 = Uk[i3], B_Uk[i3]
                yield cx.op("act", lambda e: e.copy(out=U0[:], in_=R6b[:, 0:128]), reads=[BR6], writes=[BU0])
                P0, BP0 = Pm[i3], B_Pm[i3]
                yield cx.op("dve", lambda e: e.tensor_tensor(out=tmpb[i2][:], in0=R6b[:, 0:128], in1=maskE[:, 0, :],
                                                      op=ALU.mult), reads=[BR6, B_small], writes=[B_tmpb[i2]])
                yield cx.op("dve", lambda e: e.tensor_tensor(out=P0[:], in0=identb[:], in1=tmpb[i2][:],
                                                      op=ALU.subtract), reads=[B_tmpb[i2], B_small], writes=[BP0])
                Pc, BPc = P0, BP0
                ia = i3
                for lvl in range(1, 7):
                    ib = (ia + 1) % 3
                    Pn_, BPn = Pm[ib], B_Pm[ib]
                    El, BEl = Uk[lvl % 3], B_Uk[lvl % 3]
                    yield cx.op("dve", lambda e: e.tensor_tensor(out=El[:], in0=L0[:], in1=maskE[:, 7 + lvl, :],
                                                          op=ALU.mult), reads=[BL0, B_small], writes=[BEl])
                    Ra, BRa = yield from acq(st)
                    yield mm(Ra[:, 0:128], El[:], Pc[:], [BEl, BPc], [BRa])
                    Xs, BXs = Lk[(i3 + 1 + lvl % 2) % 3], B_Lk[(i3 + 1 + lvl % 2) % 3]
                    yield cx.op("act", lambda e: e.copy(out=Xs[:], in_=Ra[:, 0:128]), reads=[BRa], writes=[BXs])
                    Rb_, BRb = yield from acq(st)
                    Rbb = Rb_[:, :].bitcast(BF16)
                    yield cx.op("pe", lambda e: e.transpose(Rbb[:, 0:128], Pc[:], identb[:]), reads=[BPc, B_small],
                          writes=[BRb])
                    yield cx.op("act", lambda e: e.copy(out=Tt[i2][:], in_=Rbb[:, 0:128]), reads=[BRb], writes=[B_Tt[i2]])
                    Rc, BRc = yield from acq(st)
                    yield mm(Rc[:, 0:128], Tt[i2][:], Xs[:], [B_Tt[i2], BXs], [BRc])
                    yield cx.op("dve", lambda e: e.tensor_tensor(out=Pn_[:], in0=Pc[:], in1=Rc[:, 0:128],
                                                          op=ALU.subtract), reads=[BRc, BPc], writes=[BPn])
                    Pc, BPc = Pn_, BPn
                    ia = ib
                yield "SPLIT"
                BS_ = B_S[h]
                Rd, BRd = yield from acq(st)
                yield mm(Rd[:, 0:128], kT_n[i2][:], Sb[h][:], [B_kT_n[i2], BS_], [BRd])
                yield cx.op("dve", lambda e: e.scalar_tensor_tensor(out=rr[i2][:], in0=Rd[:, 0:128],
                                                              scalar=G["nbeG"][:, hc], in1=bv[i2][:],
                                                              op0=ALU.mult, op1=ALU.add),
                      reads=[BRd, B_gt, B_bv[i2]], writes=[B_rr[i2]])
                Re, BRe = yield from acq(st)
                yield mm(Re[:, 0:128], Pc[:], rr[i2][:], [BPc, B_rr[i2]], [BRe])
                yield cx.op("act", lambda e: e.copy(out=vnew[i2][:], in_=Re[:, 0:128]), reads=[BRe], writes=[B_vnew[i2]])
                Rf, BRf = yield from acq(st)
                yield mm(Rf[:, 0:128], attnT[i2][:], vnew[i2][:], [B_attnT[i2], B_vnew[i2]], [BRf])
                yield cx.op("act", lambda e: e.copy(out=otmp[i2][:], in_=Rf[:, 0:128]), reads=[BRf], writes=[B_otmp[i2]])
                Rg, BRg = yield from acq(st)
                yield mm(Rg[:, 0:128], qT_n[i2][:], Sb[h][:], [B_qT_n[i2], BS_], [BRg])
                yield cx.op("dve", lambda e: e.scalar_tensor_tensor(out=oh[i2][:], in0=Rg[:, 0:128],
                                                              scalar=G["eG"][:, hc], in1=otmp[i2][:],
                                                              op0=ALU.mult, op1=ALU.add),
                      reads=[BRg, B_gt, B_otmp[i2]], writes=[B_oh[i2]])
                Rh, BRh = yield from acq(st)
                yield mm(Rh[:, 0:128], k_dec[i2][:], vnew[i2][:], [B_k_dec[i2], B_vnew[i2]], [BRh])
                yield cx.op("dve", lambda e: e.scalar_tensor_tensor(out=Sf[h][:], in0=Sf[h][:], scalar=G["egl"][:, hc],
                                                              in1=Rh[:, 0:128], op0=ALU.mult, op1=ALU.add),
                      reads=[BRh, B_gt], writes=[BS_])
                yield cx.op("act", lambda e: e.copy(out=Sb[h][:], in_=Sf[h][:]), reads=[BS_], writes=[BS_])
                yield cx.op("act", lambda e: e.activation(out=junkh[i2][:], in_=oh[i2][:], func=AF.Square,
                                                    accum_out=osqH[i2][:, 0:1]), reads=[B_oh[i2]],
                      writes=[B_junkh[i2], B_osqH[i2]])
                yield cx.op("dve", lambda e: e.tensor_scalar(out=osqH[i2][:, 1:2], in0=osqH[i2][:, 0:1], scalar1=1.0 / 128,
                                                       scalar2=EPS, op0=ALU.mult, op1=ALU.add),
                      reads=[B_osqH[i2]], writes=[B_osqH[i2]])
                yield cx.op("act", lambda e: e.activation(out=osqH[i2][:, 1:2], in_=osqH[i2][:, 1:2], func=AF.Ln),
                      reads=[B_osqH[i2]], writes=[B_osqH[i2]])
                yield cx.op("act", lambda e: e.activation(out=osqH[i2][:, 1:2], in_=osqH[i2][:, 1:2], func=AF.Exp, scale=-0.5),
                      reads=[B_osqH[i2]], writes=[B_osqH[i2]])
                yield cx.op("dve", lambda e: e.scalar_tensor_tensor(out=oh[i2][:], in0=oh[i2][:], scalar=osqH[i2][:, 1:2],
                                                              in1=onorm[:], op0=ALU.mult, op1=ALU.mult),
                      reads=[B_osqH[i2], B_small], writes=[B_oh[i2]])
                yield cx.op("dve", lambda e: e.tensor_tensor(out=oy[i2][:], in0=oh[i2][:],
                                                      in1=zs[:, s, h * 128:(h + 1) * 128], op=ALU.mult),
                      reads=[B_oh[i2], B_zs], writes=[B_oy[i2]])
                Ri, BRi = yield from acq(st)
                Rib = Ri[:, :].bitcast(BF16)
                yield cx.op("pe", lambda e: e.transpose(Rib[:, 0:128], oy[i2][:], identb[:]), reads=[B_oy[i2], B_small],
                      writes=[BRi])
                yield cx.op("act", lambda e: e.copy(out=oT_sb[i2][:], in_=Rib[:, 0:128]), reads=[BRi],
                      writes=[B_oTsb[i2]])
                yield cx.dma("sp", oT_out[h * 128:(h + 1) * 128, ch * 128:(ch + 1) * 128], oT_sb[i2][:],
                       reads=[B_oTsb[i2]])

                rel(st)
            pass

        alive = []
        nxt = 0
        newest = []
        split_seen = 0
        while alive or nxt < 4:
            if nxt < 4 and (not alive or split_seen == HG):
                s_ = nxt
                newest = [head_gen(h, s_, slice(s_ * 128, (s_ + 1) * 128), ti * 4 + s_) for h in range(HG)]
                alive.extend(newest)
                split_seen = 0
                nxt += 1
            for g_ in list(alive):
                try:
                    r_ = next(g_)
                    if r_ == "SPLIT" and g_ in newest:
                        split_seen += 1
                except StopIteration:
                    alive.remove(g_)


def gdn_consts():
    p = np.arange(128)[:, None]
    f = np.arange(128)[None, :]
    UT = (p <= f).astype(np.float32)
    return {
        "UTf": UT, "nUTf": -UT,
        "lowneg": np.where(p > f, 0.0, NEG).astype(np.float32),
        "upneg": np.where(p <= f, 0.0, NEG).astype(np.float32),
        "ident": np.eye(128, dtype=ml_dtypes.bfloat16),
        "maskE": gdn_masks(),
    }


def gdn_masks():
    p = np.arange(128)[:, None]
    f = np.arange(128)[None, :]
    m = np.zeros((128, 14, 128), np.float32)
    for l in range(7):
        b = 2 ** (l + 1)
        same = (p // b) == (f // b)
        low = same & ((p % b) >= b // 2) & ((f % b) < b // 2)
        m[:, 7 + l, :] = low
        m[:, l, :] = low.T
    return m.astype(ml_dtypes.bfloat16)


def gdn_inputs(h_b, gdn_norm, w_in, conv_w, a_log, dt_bias, o_norm, g):
    hs = slice(g * 512, (g + 1) * 512)
    cols = [w_in[:, 0:1024][:, hs], w_in[:, 1024:2048][:, hs], w_in[:, 2048:3072][:, hs],
            w_in[:, 3072:4096][:, hs], w_in[:, 4096 + g * HG:4096 + (g + 1) * HG],
            w_in[:, 4104 + g * HG:4104 + (g + 1) * HG]]
    cw = np.stack([conv_w[:, w3 * 1024:(w3 + 1) * 1024][:, hs] for w3 in range(3)], 0)
    cw = cw.reshape(3, 4, HG, 128).transpose(3, 0, 2, 1).reshape(128, 12, 4)
    m = {"x": h_b, "w_in": np.ascontiguousarray(np.concatenate(cols, 1)),
         "gcol": np.ascontiguousarray(gdn_norm.reshape(8, 128).T),
         "convw": np.ascontiguousarray(cw),
         "alog": np.ascontiguousarray(np.broadcast_to(a_log[g * HG:(g + 1) * HG][None], (128, HG))),
         "dtb": np.ascontiguousarray(np.broadcast_to(dt_bias[g * HG:(g + 1) * HG][None], (128, HG))),
         "onorm": np.ascontiguousarray(o_norm)}
    m.update(gdn_consts())
    return m


def allgather(cx, pairs, groups):
    cx.barrier()
    E = cx.eng["pool"]
    for src, dst in pairs:
        E.e.collective_compute("AllGather", ALU.bypass, replica_groups=groups, ins=[src], outs=[dst]).then_inc(
            cx.ccsem.h, 1)
        cx.ccsem.total += 1
        cx._wait(E, [(cx.ccsem, cx.ccsem.total)])
    for en in cx.eng.values():
        cx._wait(en, [(cx.ccsem, cx.ccsem.total)])


def build_fused(S_=SEQ, n_cores=8, upto=99):
    nc = bass.Bass("TRN2", target_bir_lowering=False)
    H_ = S_ // 2
    groups = [[2 * i, 2 * i + 1] for i in range(n_cores // 2)]

    def din(name, shape, dt=F32):
        return nc.dram_tensor(name, list(shape), dt, kind="ExternalInput").ap()

    def dint(name, shape, dt):
        return nc.dram_tensor(name, list(shape), dt).ap()

    x_full = din("x_full", [S_, D])
    x_half = din("x_half", [HALO + H_, D])
    m01 = din("m01", [128, 2])
    identb = din("identb", [128, 128], BF16)
    io_a = {
        "x": x_full, "w_qkv": din("a_w_qkv", [D, 1536]), "gcol": din("a_gcol", [128, 8]),
        "ident": identb, "identf": din("identf", [128, 128]),
        "negpad": din("negpad", [128, 16, 4, 32]), "farind": din("farind", [128, 16, 32]),
        "relb31": din("relb31", [128, HG]), "esel": din("esel", [32, 32, 128], BF16),
        "tt": din("tt", [HG, 128, TTW], BF16),
        "oT_out": dint("oT_mine1", [HG * 128, S_], BF16),
    }
    io_a.update(attn_scratch(nc, S_))
    oT_full1 = dint("oT_full1", [2 * HG * 128, S_], BF16)
    h1_half = dint("h1_half", [H_, D], F32)
    h1_full = dint("h1_full", [S_, D], F32)
    io_p0 = {
        "hin_main": x_half[HALO:, :], "hin_halo": x_half[0:HALO, :], "oT_full": oT_full1, "half": H_,
        "hout": h1_half, "m01": m01, "ident": identb,
        "w_o": din("p0_w_o", [D, D]), "w_up": din("p0_w_up", [D, 2 * DFF]), "w_down": din("p0_w_down", [DFF, D]),
        "gcol": din("p0_gcol", [128, 8]), "cw": din("p0_cw", [128, NFC, 3]), "cb": din("p0_cb", [128, NFC]),
    }
    nk = H_ // 512
    io_g = {
        "x_tile": lambda ti: h1_full[(ti % nk) * 1024 + (ti // nk) * 512:(ti % nk) * 1024 + (ti // nk) * 512 + 512, :],
        "x": h1_full, "w_in": din("g_w_in", [D, 2056]), "gcol": din("g_gcol", [128, 8]),
        "convw": din("g_convw", [128, 12, 4]), "alog": din("g_alog", [128, HG]), "dtb": din("g_dtb", [128, HG]),
        "onorm": din("g_onorm", [128]), "ident": identb,
        "UTf": din("UTf", [128, 128]), "nUTf": din("nUTf", [128, 128]),
        "lowneg": din("lowneg", [128, 128]), "upneg": din("upneg", [128, 128]),
        "maskE": din("maskE", [128, 14, 128], BF16),
        "oT_out": dint("oT_mine2", [HG * 128, S_], BF16),
    }
    oT_full2 = dint("oT_full2", [2 * HG * 128, S_], BF16)
    io_p1 = {
        "hin_main": h1_half, "hin_halo": h1_full[(nk - 1) * 1024 + 512 - HALO:(nk - 1) * 1024 + 512, :],
        "halo_mask": True, "oT_full": oT_full2,
        "half": H_, "m01": m01, "ident": identb,
        "hout": nc.dram_tensor("out", [H_, D], F32, kind="ExternalOutput").ap(),
        "w_o": din("p1_w_o", [D, D]), "w_up": din("p1_w_up", [D, 2 * DFF]), "w_down": din("p1_w_down", [DFF, D]),
        "gcol": din("p1_gcol", [128, 8]), "cw": din("p1_cw", [128, NFC, 3]), "cb": din("p1_cb", [128, NFC]),
        "gfin": din("gfin", [D]),
    }
    with contextlib.ExitStack() as es:
        cx = Ctx(nc, es)
        cx.ccsem = Sem(es.enter_context(nc.semaphore("ccsem")))

        nph = [0]

        def phase(fn, *a, **kw):
            with contextlib.ExitStack() as es_p:
                cx.es = es_p
                cx.pfx = "p%d_" % nph[0]
                nph[0] += 1
                fn(cx, *a, **kw)
                cx.barrier()

        def ag_heads(mine, full):
            allgather(cx, [(mine[h * 128:(h + 1) * 128, :], full[h * 256:(h + 1) * 256, :]) for h in range(HG)],
                      groups)

        steps = [
            lambda: phase(phase_attn_a, io_a, S_),
            lambda: phase(phase_attn_b, io_a, S_),
            lambda: ag_heads(io_a["oT_out"], oT_full1),
            lambda: phase(phase_post, io_p0, False, n_tok=H_),
            lambda: allgather(cx, [(h1_half[k * 512:(k + 1) * 512, :], h1_full[k * 1024:(k + 1) * 1024, :])
                                   for k in range(nk)], groups),
            lambda: phase(phase_gdn, io_g, S_),
            lambda: ag_heads(io_g["oT_out"], oT_full2),
            lambda: phase(phase_post, io_p1, True, n_tok=H_),
        ]
        for st in steps[:upto]:
            st()
        cx.barrier(final=True)
    return nc


def fused_inputs(x_b, r, rel_bias, attn_norm, attn_w_qkv, attn_w_o, gdn_norm, gdn_w_in, gdn_conv_w, gdn_a_log,
                 gdn_dt_bias, gdn_o_norm, gdn_w_o, ffn_norm, ffn_w_up, ffn_conv_w, ffn_conv_b, ffn_w_down,
                 final_norm):
    S_ = x_b.shape[0]
    H_ = S_ // 2
    m = {"x_full": x_b, "m01": np.ascontiguousarray(np.broadcast_to(np.array([1.0 - r, r], np.float32), (128, 2)))}
    if r == 0:
        m["x_half"] = np.concatenate([np.zeros((HALO, D), np.float32), x_b[:H_]], 0)
    else:
        m["x_half"] = np.ascontiguousarray(x_b[H_ - HALO:])
    a = attn_inputs(x_b, attn_norm, attn_w_qkv, rel_bias, r)
    m.update({"a_w_qkv": a["w_qkv"], "a_gcol": a["gcol"], "identb": a["ident"], "identf": a["identf"],
              "negpad": a["negpad"], "farind": a["farind"], "relb31": a["relb31"], "esel": a["esel"], "tt": a["tt"]})
    p0 = post_small_inputs(ffn_norm[0], ffn_conv_w[0], ffn_conv_b[0])
    perm = np.concatenate([np.arange(128) + ((c % 2) * HG + c // 2) * 128 for c in range(8)])
    attn_w_o = np.ascontiguousarray(attn_w_o[perm])
    gdn_w_o = np.ascontiguousarray(gdn_w_o[perm])
    m.update({"p0_w_o": attn_w_o, "p0_w_up": ffn_w_up[0], "p0_w_down": ffn_w_down[0],
              "p0_gcol": p0["gcol"], "p0_cw": p0["cw"], "p0_cb": p0["cb"]})
    g = gdn_inputs(x_b, gdn_norm, gdn_w_in, gdn_conv_w, gdn_a_log, gdn_dt_bias, gdn_o_norm, r)
    m.update({"g_w_in": g["w_in"], "g_gcol": g["gcol"], "g_convw": g["convw"], "g_alog": g["alog"],
              "g_dtb": g["dtb"], "g_onorm": g["onorm"], "UTf": g["UTf"], "nUTf": g["nUTf"],
              "lowneg": g["lowneg"], "upneg": g["upneg"], "maskE": g["maskE"]})
    p1 = post_small_inputs(ffn_norm[1], ffn_conv_w[1], ffn_conv_b[1])
    m.update({"p1_w_o": gdn_w_o, "p1_w_up": ffn_w_up[1], "p1_w_down": ffn_w_down[1],
              "p1_gcol": p1["gcol"], "p1_cw": p1["cw"], "p1_cb": p1["cb"], "gfin": final_norm})
    return m


_PROG = {}


def kernel(x, rel_bias, attn_norm, attn_w_qkv, attn_w_o, gdn_norm, gdn_w_in, gdn_conv_w, gdn_a_log,
           gdn_dt_bias, gdn_o_norm, gdn_w_o, ffn_norm, ffn_w_up, ffn_conv_w, ffn_conv_b, ffn_w_down, final_norm):
    f32 = lambda a: np.ascontiguousarray(np.asarray(a, dtype=np.float32))
    x = f32(x)
    B, S_ = x.shape[0], x.shape[1]
    n_cores = 2 * B
    key = (S_, n_cores)
    if key not in _PROG:
        _PROG[key] = build_fused(S_, n_cores)
    nc = _PROG[key]
    w = dict(rel_bias=f32(rel_bias), attn_norm=f32(attn_norm[0]), attn_w_qkv=f32(attn_w_qkv[0]),
             attn_w_o=f32(attn_w_o[0]), gdn_norm=f32(gdn_norm[0]), gdn_w_in=f32(gdn_w_in[0]),
             gdn_conv_w=f32(gdn_conv_w[0]), gdn_a_log=f32(gdn_a_log[0]), gdn_dt_bias=f32(gdn_dt_bias[0]),
             gdn_o_norm=f32(gdn_o_norm[0]), gdn_w_o=f32(gdn_w_o[0]), ffn_norm=f32(ffn_norm),
             ffn_w_up=f32(ffn_w_up), ffn_conv_w=f32(ffn_conv_w), ffn_conv_b=f32(ffn_conv_b),
             ffn_w_down=f32(ffn_w_down), final_norm=f32(final_norm))
    maps = [fused_inputs(x[b], r, **w) for b in range(B) for r in range(2)]
    res = run_bass_kernel_spmd(nc, maps, core_ids=list(range(n_cores))).results
    out = np.stack([np.concatenate([np.asarray(res[2 * b + r]["out"]) for r in range(2)], 0) for b in range(B)], 0)
    return out.astype(np.float32)
```
